# Optimizing a Trainium2 kernel written in Bass

```python
import jax, jax.numpy as jnp
from jax import lax
import numpy as np

D_MODEL = 1024
BATCH = 4
SEQ = 8192
DEPTH = 2

ATT_HEADS = 16
ATT_KV_HEADS = 2
ATT_HEAD_DIM = 64
WINDOW = 128
ATT_BLOCK = 128
SSD_EXPAND = 2
SSD_D_INNER = SSD_EXPAND * D_MODEL
SSD_HEAD_DIM = 64
SSD_HEADS = SSD_D_INNER // SSD_HEAD_DIM
SSD_GROUPS = 4
SSD_STATE = 128
SSD_CONV = 4
SSD_CHUNK = 128
FFN_HIDDEN = -(-8 * D_MODEL // (3 * 256)) * 256

LN_EPS = 1e-5
RMS_EPS = 1e-5
DEEPNORM_ALPHA = (2 * DEPTH) ** 0.25
DEEPNORM_BETA = (8 * DEPTH) ** -0.25

Q_DIM = ATT_HEADS * ATT_HEAD_DIM
KV_DIM = ATT_KV_HEADS * ATT_HEAD_DIM
BC_DIM = SSD_GROUPS * SSD_STATE
CONV_DIM = SSD_D_INNER + 2 * BC_DIM
IN_SIZES = (Q_DIM, KV_DIM, KV_DIM, SSD_D_INNER, SSD_D_INNER, BC_DIM, BC_DIM, SSD_HEADS, 2 * D_MODEL)
IN_DIM = sum(IN_SIZES)

kernel_name = 'hybrid_ssd_swa_sink_alibi_deepnorm'


def _split(t, sizes):
    offs = np.cumsum(sizes)[:-1].tolist()
    return jnp.split(t, offs, axis=-1)


def layer_norm(x, g, b):
    xf = x.astype(jnp.float32)
    mu = jnp.mean(xf, axis=-1, keepdims=True)
    var = jnp.mean(jnp.square(xf - mu), axis=-1, keepdims=True)
    return ((xf - mu) * lax.rsqrt(var + LN_EPS) * g + b).astype(x.dtype)


def grouped_rms_norm(y, w):
    yg = y.reshape(*y.shape[:-1], SSD_GROUPS, -1)
    yg = yg * lax.rsqrt(jnp.mean(jnp.square(yg), axis=-1, keepdims=True) + RMS_EPS)
    return yg.reshape(y.shape) * w


def causal_depthwise_conv(u, w, b):
    c = u.shape[-1]
    out = lax.conv_general_dilated(
        u, w[:, None, :].astype(u.dtype), window_strides=(1,),
        padding=[(SSD_CONV - 1, 0)], dimension_numbers=('NWC', 'WIO', 'NWC'),
        feature_group_count=c)
    return out + b


def segsum(a):
    t = a.shape[-1]
    cs = jnp.cumsum(a, axis=-1)
    diff = cs[..., :, None] - cs[..., None, :]
    mask = jnp.tril(jnp.ones((t, t), dtype=bool))
    return jnp.where(mask, diff, -jnp.inf)


def ssd_chunked_scan(xh, dt, a, b_ssm, c_ssm):
    bsz, seqlen, nh, hp = xh.shape
    ng, ns = b_ssm.shape[2], b_ssm.shape[3]
    ne = nh // ng
    nc = seqlen // SSD_CHUNK
    T = SSD_CHUNK
    X = (xh.astype(jnp.float32) * dt[..., None]).reshape(bsz, nc, T, ng, ne, hp)
    dA = jnp.moveaxis((dt * a).reshape(bsz, nc, T, ng, ne), 2, -1)
    a_cum = jnp.cumsum(dA, axis=-1)
    Bc = b_ssm.astype(jnp.float32).reshape(bsz, nc, T, ng, ns)
    Cc = c_ssm.astype(jnp.float32).reshape(bsz, nc, T, ng, ns)
    Lmat = jnp.exp(segsum(dA))
    CB = jnp.einsum('bclgn,bcsgn->bcgls', Cc, Bc)
    y_diag = jnp.einsum('bcgels,bcsgep->bclgep', Lmat * CB[:, :, :, None], X)
    decay_states = jnp.exp(a_cum[..., -1:] - a_cum)
    states = jnp.einsum('bclgn,bcgel,bclgep->bcgepn', Bc, decay_states, X)
    chunk_decay = jnp.exp(a_cum[..., -1])

    def step(h, inp):
        s_c, d_c = inp
        return h * d_c[..., None, None] + s_c, h

    h0 = jnp.zeros((bsz, ng, ne, hp, ns), jnp.float32)
    _, h_in = lax.scan(step, h0, (jnp.moveaxis(states, 1, 0), jnp.moveaxis(chunk_decay, 1, 0)))
    h_in = jnp.moveaxis(h_in, 0, 1)
    y_off = jnp.einsum('bclgn,bcgepn,bcgel->bclgep', Cc, h_in, jnp.exp(a_cum))
    return (y_diag + y_off).reshape(bsz, seqlen, nh, hp)


def alibi_slopes(n_heads):
    return jnp.exp2(-8.0 * jnp.arange(1, n_heads + 1, dtype=jnp.float32) / n_heads)


def sliding_window_sink_attention(q, k, v, sinks):
    bsz, seqlen, nh, hd = q.shape
    nkv = k.shape[2]
    ng = nh // nkv
    nb = seqlen // ATT_BLOCK
    qb = (q * (hd ** -0.5)).reshape(bsz, nb, ATT_BLOCK, nkv, ng, hd)
    pad = ((0, 0), (ATT_BLOCK, 0), (0, 0), (0, 0))
    kb = jnp.pad(k, pad).reshape(bsz, nb + 1, ATT_BLOCK, nkv, hd)
    vb = jnp.pad(v, pad).reshape(bsz, nb + 1, ATT_BLOCK, nkv, hd)
    k_band = jnp.concatenate([kb[:, :-1], kb[:, 1:]], axis=2)
    v_band = jnp.concatenate([vb[:, :-1], vb[:, 1:]], axis=2)
    scores = jnp.einsum('bnqkgd,bnskd->bnkgqs', qb, k_band).astype(jnp.float32)
    qi = jnp.arange(ATT_BLOCK)
    kj = jnp.arange(2 * ATT_BLOCK)
    rel = qi[:, None] + ATT_BLOCK - kj[None, :]
    key_pos = jnp.arange(nb)[:, None] * ATT_BLOCK - ATT_BLOCK + kj[None, :]
    valid = (rel >= 0)[None] & (rel < WINDOW)[None] & (key_pos >= 0)[:, None, :]
    slopes = alibi_slopes(nh).reshape(nkv, ng)
    bias = -slopes[:, :, None, None] * rel.astype(jnp.float32)
    scores = jnp.where(valid[None, :, None, None], scores + bias, -jnp.inf)
    sink = sinks.astype(jnp.float32).reshape(1, 1, nkv, ng, 1, 1)
    m = jnp.maximum(jnp.max(scores, axis=-1, keepdims=True), sink)
    p = jnp.exp(scores - m)
    p = p / (jnp.sum(p, axis=-1, keepdims=True) + jnp.exp(sink - m))
    out = jnp.einsum('bnkgqs,bnskd->bnqkgd', p.astype(v.dtype), v_band)
    return out.reshape(bsz, seqlen, nh * hd)


def token_mixer(h, w_in, conv_w, conv_b, dt_bias, a_log, d_skip, ssd_norm_w, att_sinks,
                w_ssd_out, w_att_out, w_mix_out):
    bsz, seqlen, _ = h.shape
    proj = h @ w_in
    q, k, v, z, xs, b_ssm, c_ssm, dt_raw, gate_logits = _split(proj, IN_SIZES)
    xbc = jax.nn.silu(causal_depthwise_conv(jnp.concatenate([xs, b_ssm, c_ssm], axis=-1), conv_w, conv_b))
    xs, b_ssm, c_ssm = _split(xbc, (SSD_D_INNER, BC_DIM, BC_DIM))
    xh = xs.reshape(bsz, seqlen, SSD_HEADS, SSD_HEAD_DIM)
    dt = jax.nn.softplus(dt_raw.astype(jnp.float32) + dt_bias)
    a = -jnp.exp(a_log.astype(jnp.float32))
    y = ssd_chunked_scan(xh, dt, a,
                         b_ssm.reshape(bsz, seqlen, SSD_GROUPS, SSD_STATE),
                         c_ssm.reshape(bsz, seqlen, SSD_GROUPS, SSD_STATE))
    y = y + d_skip[:, None] * xh
    y = y.reshape(bsz, seqlen, SSD_D_INNER) * jax.nn.silu(z.astype(jnp.float32))
    y_a = grouped_rms_norm(y, ssd_norm_w).astype(h.dtype) @ w_ssd_out
    att = sliding_window_sink_attention(
        q.reshape(bsz, seqlen, ATT_HEADS, ATT_HEAD_DIM),
        k.reshape(bsz, seqlen, ATT_KV_HEADS, ATT_HEAD_DIM),
        v.reshape(bsz, seqlen, ATT_KV_HEADS, ATT_HEAD_DIM), att_sinks)
    y_b = att @ w_att_out
    g_a, g_b = jnp.split(jax.nn.sigmoid(gate_logits), 2, axis=-1)
    return (g_a * y_a + g_b * y_b) @ w_mix_out


def swiglu_ffn(h, w_gate, w_up, w_down):
    return (jax.nn.silu(h @ w_gate) * (h @ w_up)) @ w_down


def setup_inputs(seed: int = 0) -> dict:
    key = jax.random.key(seed)
    ks = jax.random.split(key, 24)
    f32 = jnp.float32

    def nrm(k, shape, scale):
        return jax.random.normal(k, shape, f32) * scale

    x = nrm(ks[0], (BATCH, SEQ, D_MODEL), 1.0)
    ln_in_g = 1.0 + nrm(ks[1], (D_MODEL,), 0.02)
    ln_in_b = nrm(ks[2], (D_MODEL,), 0.02)
    col_scale = jnp.concatenate([
        jnp.ones((Q_DIM + KV_DIM,), f32), jnp.full((KV_DIM,), DEEPNORM_BETA, f32),
        jnp.ones((SSD_D_INNER,), f32), jnp.full((SSD_D_INNER,), DEEPNORM_BETA, f32),
        jnp.ones((2 * BC_DIM + SSD_HEADS + 2 * D_MODEL,), f32)])
    w_in = nrm(ks[3], (DEPTH, D_MODEL, IN_DIM), D_MODEL ** -0.5) * col_scale
    conv_w = nrm(ks[4], (DEPTH, SSD_CONV, CONV_DIM), SSD_CONV ** -0.5)
    conv_b = nrm(ks[5], (DEPTH, CONV_DIM), 0.01)
    dt0 = jnp.exp(jax.random.uniform(ks[6], (DEPTH, SSD_HEADS), f32)
                  * (jnp.log(0.1) - jnp.log(0.001)) + jnp.log(0.001))
    dt_bias = dt0 + jnp.log(-jnp.expm1(-dt0))
    a_log = jnp.log(jax.random.uniform(ks[7], (DEPTH, SSD_HEADS), f32, 1.0, 16.0))
    d_skip = 1.0 + nrm(ks[8], (DEPTH, SSD_HEADS), 0.1)
    ssd_norm_w = 1.0 + nrm(ks[9], (DEPTH, SSD_D_INNER), 0.02)
    att_sinks = nrm(ks[10], (DEPTH, ATT_HEADS), 0.5)
    w_ssd_out = nrm(ks[11], (DEPTH, SSD_D_INNER, D_MODEL), SSD_D_INNER ** -0.5 * DEEPNORM_BETA)
    w_att_out = nrm(ks[12], (DEPTH, Q_DIM, D_MODEL), Q_DIM ** -0.5 * DEEPNORM_BETA)
    w_mix_out = nrm(ks[13], (DEPTH, D_MODEL, D_MODEL), D_MODEL ** -0.5 * DEEPNORM_BETA)
    ln_mix_g = 1.0 + nrm(ks[14], (DEPTH, D_MODEL), 0.02)
    ln_mix_b = nrm(ks[15], (DEPTH, D_MODEL), 0.02)
    w_ffn_gate = nrm(ks[16], (DEPTH, D_MODEL, FFN_HIDDEN), D_MODEL ** -0.5 * DEEPNORM_BETA)
    w_ffn_up = nrm(ks[17], (DEPTH, D_MODEL, FFN_HIDDEN), D_MODEL ** -0.5 * DEEPNORM_BETA)
    w_ffn_down = nrm(ks[18], (DEPTH, FFN_HIDDEN, D_MODEL), FFN_HIDDEN ** -0.5 * DEEPNORM_BETA)
    ln_ffn_g = 1.0 + nrm(ks[19], (DEPTH, D_MODEL), 0.02)
    ln_ffn_b = nrm(ks[20], (DEPTH, D_MODEL), 0.02)
    return {'x': x, 'ln_in_g': ln_in_g, 'ln_in_b': ln_in_b, 'w_in': w_in,
            'conv_w': conv_w, 'conv_b': conv_b, 'dt_bias': dt_bias, 'a_log': a_log,
            'd_skip': d_skip, 'ssd_norm_w': ssd_norm_w, 'att_sinks': att_sinks,
            'w_ssd_out': w_ssd_out, 'w_att_out': w_att_out, 'w_mix_out': w_mix_out,
            'ln_mix_g': ln_mix_g, 'ln_mix_b': ln_mix_b, 'w_ffn_gate': w_ffn_gate,
            'w_ffn_up': w_ffn_up, 'w_ffn_down': w_ffn_down,
            'ln_ffn_g': ln_ffn_g, 'ln_ffn_b': ln_ffn_b}


def reference(x, ln_in_g, ln_in_b, w_in, conv_w, conv_b, dt_bias, a_log, d_skip, ssd_norm_w,
              att_sinks, w_ssd_out, w_att_out, w_mix_out, ln_mix_g, ln_mix_b,
              w_ffn_gate, w_ffn_up, w_ffn_down, ln_ffn_g, ln_ffn_b):
    h = layer_norm(x, ln_in_g, ln_in_b)
    for l in range(DEPTH):
        mix = token_mixer(h, w_in[l], conv_w[l], conv_b[l], dt_bias[l], a_log[l], d_skip[l],
                          ssd_norm_w[l], att_sinks[l], w_ssd_out[l], w_att_out[l], w_mix_out[l])
        h = layer_norm(DEEPNORM_ALPHA * h + mix, ln_mix_g[l], ln_mix_b[l])
        ffn = swiglu_ffn(h, w_ffn_gate[l], w_ffn_up[l], w_ffn_down[l])
        h = layer_norm(DEEPNORM_ALPHA * h + ffn, ln_ffn_g[l], ln_ffn_b[l])
    return h
```

```python
import contextlib
import types
import numpy as np
import concourse.bass as bass
import concourse.mybir as mybir
from concourse.bass_utils import run_bass_kernel_spmd

F32 = mybir.dt.float32
BF16 = mybir.dt.bfloat16
AF = mybir.ActivationFunctionType
ALU = mybir.AluOpType
AX = mybir.AxisListType

D = 1024
KD = 8
T = 128
DEPTH = 2
SEQ = 8192
BATCH = 4
DI = 2048
NHS = 32
HP = 64
NGR = 4
FF = 2816
KF = 22
AH = 16
O_Q, O_K, O_V, O_Z, O_XS, O_B, O_C, O_DT, O_GA, O_GB = 0, 1024, 1152, 1280, 3328, 5376, 5888, 6400, 6432, 7456
ALPHA = float((2 * DEPTH) ** 0.25)
LN_EPS = 1e-5
RMS_EPS = 1e-5
NEG = -30000.0
SLAB = 4096
NW = 3

ENGS = ("pe", "act", "dve", "pool", "sp")


class Buf:
    __slots__ = ("name", "last_writer", "readers", "excl")

    def __init__(self, name, excl=False):
        self.name = name
        self.last_writer = None
        self.readers = []
        self.excl = excl


class Op:
    __slots__ = ("eng", "fn", "deps", "is_dma", "idx", "signal", "sig_val", "sem", "dma_val", "dma_prev")

    def __init__(self, eng, fn, is_dma):
        self.eng = eng
        self.fn = fn
        self.deps = []
        self.is_dma = is_dma
        self.idx = 0
        self.signal = False
        self.sig_val = 0
        self.sem = None
        self.dma_val = 0
        self.dma_prev = None


def _freeze(fn):
    if fn.__closure__ is None:
        return fn
    cells = []
    for c in fn.__closure__:
        try:
            cells.append(types.CellType(c.cell_contents))
        except ValueError:
            cells.append(c)
    return types.FunctionType(fn.__code__, fn.__globals__, fn.__name__, fn.__defaults__, tuple(cells))


class Prog:
    NDMASEM = 6
    SEM_WRAP = 30000

    def __init__(self, nc):
        self.nc = nc
        self.ops = {e: [] for e in ENGS}
        self.dma_hist = {e: [] for e in ENGS}

    halt = False

    def op(self, eng, fn, reads=(), writes=(), dma=False):
        if self.halt:
            return None
        o = Op(eng, _freeze(fn), dma)
        deps = []
        for b in reads:
            if b.last_writer is not None:
                deps.append(b.last_writer)
            if b.excl:
                deps.extend(r for r in b.readers if r.eng != eng)
        for b in writes:
            if b.last_writer is not None:
                deps.append(b.last_writer)
            deps.extend(b.readers)
        for b in reads:
            b.readers.append(o)
        for b in writes:
            b.last_writer = o
            b.readers = []
        seen = set()
        for d in deps:
            if d is o or id(d) in seen:
                continue
            seen.add(id(d))
            if (not d.is_dma) and d.eng == "pe" and eng == "pe" and not dma:
                continue
            o.deps.append(d)
            if not d.is_dma:
                d.signal = True
        if dma:
            hist = self.dma_hist[eng]
            o.idx = len(hist)
            if o.idx >= self.NDMASEM:
                o.dma_prev = hist[o.idx - self.NDMASEM]
            hist.append(o)
        self.ops[eng].append(o)
        return o

    def emit(self, st):
        nc = self.nc
        esem, dsem = {}, {}
        for e in ENGS:
            n = sum(1 for o in self.ops[e] if o.signal and not o.is_dma)
            esem[e] = [st.enter_context(nc.semaphore(f"s_{e}_{k}")) for k in range(max(1, -(-n // self.SEM_WRAP)))]
            if self.dma_hist[e]:
                dsem[e] = [st.enter_context(nc.semaphore(f"d_{e}_{k}")) for k in range(self.NDMASEM)]
        for e in ENGS:
            c = 0
            for o in self.ops[e]:
                if o.is_dma:
                    o.sem = dsem[e][o.idx % self.NDMASEM]
                    o.dma_val = 16 * (o.idx // self.NDMASEM + 1)
                elif o.signal:
                    o.sem = esem[e][c // self.SEM_WRAP]
                    o.sig_val = c % self.SEM_WRAP + 1
                    c += 1
        block = st.enter_context(nc.Block())

        def run(engname, engobj):
            waited = {}
            for o in self.ops[engname]:
                ws = {}
                deps = o.deps if o.dma_prev is None else o.deps + [o.dma_prev]
                for d in deps:
                    val = d.dma_val if d.is_dma else d.sig_val
                    key = id(d.sem)
                    if waited.get(key, 0) >= val:
                        continue
                    if key not in ws or ws[key][1] < val:
                        ws[key] = (d.sem, val)
                for key, (sem, val) in ws.items():
                    engobj.wait_ge(sem, val)
                    waited[key] = val
                ins = o.fn(engobj)
                if o.is_dma:
                    ins.then_inc(o.sem, 16)
                elif o.signal:
                    ins.then_inc(o.sem, 1)

        @block.tensor
        def _(e):
            run("pe", e)

        @block.scalar
        def _(e):
            run("act", e)

        @block.vector
        def _(e):
            run("dve", e)

        @block.gpsimd
        def _(e):
            run("pool", e)

        @block.sync
        def _(e):
            run("sp", e)


def _blk(W, col0, ncols, kc):
    return np.ascontiguousarray(
        W[:kc * 128, col0:col0 + ncols].reshape(kc, 128, ncols).transpose(1, 0, 2).reshape(128, kc * ncols))


def slab_plan():
    slabs = []
    b1 = [("q", j) for j in range(8)] + [("kd", kv) for kv in range(2)] + [("xbc", cb) for cb in range(24)]
    for i in range(0, len(b1), 4):
        slabs.append([(t, a, 128, 8) for (t, a) in b1[i:i + 4]])
    for s in range(4):
        slabs.append([("z", s, 512, 8)])
    slabs.append([("vdt", 0, 160, 8)])
    for j in range(8):
        slabs.append([("ga", j, 128, 8), ("gb", j, 128, 8), ("ao", j, 128, 8)])
        slabs.append([("so", j, 128, 16)])
    for i in range(0, 8, 4):
        slabs.append([("mo", j, 128, 8) for j in range(i, i + 4)])
    for i in range(0, 22, 2):
        slabs.append([("fg", i, 128, 8), ("fu", i, 128, 8), ("fg", i + 1, 128, 8), ("fu", i + 1, 128, 8)])
    for j in range(8):
        slabs.append([("fd", j, 128, 22)])
    return slabs


def slab_sizes():
    return [sum(nc_ * kc for (_, _, nc_, kc) in s) for s in slab_plan()]


def build_wstream(l, w_in, w_ssd_out, w_att_out, w_mix_out, w_ffn_gate, w_ffn_up, w_ffn_down):
    wi = w_in[l]
    kd = [np.concatenate([wi[:, O_K + kv * 64:O_K + kv * 64 + 64]] * 2, axis=1) for kv in range(2)]
    vdt = np.concatenate([wi[:, O_V:O_V + 128], wi[:, O_DT:O_DT + 32]], axis=1)
    parts = []
    for s in slab_plan():
        for (t, a, ncols, kc) in s:
            if t == "q":
                parts.append(_blk(wi, O_Q + a * 128, 128, 8))
            elif t == "kd":
                parts.append(_blk(kd[a], 0, 128, 8))
            elif t == "xbc":
                parts.append(_blk(wi, O_XS + a * 128, 128, 8))
            elif t == "z":
                parts.append(_blk(wi, O_Z + a * 512, 512, 8))
            elif t == "vdt":
                parts.append(_blk(vdt, 0, 160, 8))
            elif t == "ga":
                parts.append(_blk(wi, O_GA + a * 128, 128, 8))
            elif t == "gb":
                parts.append(_blk(wi, O_GB + a * 128, 128, 8))
            elif t == "ao":
                parts.append(_blk(w_att_out[l], a * 128, 128, 8))
            elif t == "so":
                parts.append(_blk(w_ssd_out[l], a * 128, 128, 16))
            elif t == "mo":
                parts.append(_blk(w_mix_out[l], a * 128, 128, 8))
            elif t == "fg":
                parts.append(_blk(w_ffn_gate[l], a * 128, 128, 8))
            elif t == "fu":
                parts.append(_blk(w_ffn_up[l], a * 128, 128, 8))
            elif t == "fd":
                parts.append(_blk(w_ffn_down[l], a * 128, 128, 22))
    return np.ascontiguousarray(np.concatenate(parts, axis=1), dtype=np.float32)


PL_CONVW, PL_CONVB, PL_DTB, PL_ALOG, PL_DSK, PL_SINK, PL_NORMW, PL_LMG, PL_LMB, PL_LFG, PL_LFB, PL_N = \
    0, 96, 120, 152, 184, 216, 232, 248, 256, 264, 272, 280
PG_N = 16


def _fm(v):
    return np.ascontiguousarray(v.reshape(-1, 128).T)


def _bc(v):
    return np.ascontiguousarray(np.broadcast_to(v[None, :], (128, v.shape[0])))


def build_params(inp):
    cols = [_fm(inp["ln_in_g"]), _fm(inp["ln_in_b"])]
    for l in range(DEPTH):
        cw = inp["conv_w"][l]
        cwl = cw.T.reshape(24, 128, 4).transpose(1, 0, 2).reshape(128, 96)
        cols += [cwl, _fm(inp["conv_b"][l]), _bc(inp["dt_bias"][l]), _bc(inp["a_log"][l]), _bc(inp["d_skip"][l]),
                 _bc(inp["att_sinks"][l]), _fm(inp["ssd_norm_w"][l]), _fm(inp["ln_mix_g"][l]), _fm(inp["ln_mix_b"][l]),
                 _fm(inp["ln_ffn_g"][l]), _fm(inp["ln_ffn_b"][l])]
    return np.ascontiguousarray(np.concatenate(cols, axis=1), dtype=np.float32)


C_ID, C_U, C_L, C_ONESD, C_ONES, C_R0, C_M, C_M0, C_N = 0, 128, 256, 384, 512, 640, 896, 1152, 1408


def build_consts():
    i = np.arange(128)
    ident = (i[:, None] == i[None, :]).astype(np.float32)
    U = (i[:, None] <= i[None, :]).astype(np.float32)
    Lm = (i[:, None] > i[None, :]).astype(np.float32)
    onesd = np.full((128, 128), 1.0 / D, np.float32)
    ones = np.ones((128, 128), np.float32)
    s = np.arange(256)
    rel = i[:, None] + 128 - s[None, :]
    R0 = (-rel).astype(np.float32)
    valid = (rel >= 0) & (rel < 128)
    M = np.where(valid, 0.0, NEG).astype(np.float32)
    M0 = np.where(valid & (s[None, :] >= 128), 0.0, NEG).astype(np.float32)
    return np.ascontiguousarray(np.concatenate([ident, U, Lm, onesd, ones, R0, M, M0], axis=1))


SLOPES = [float(2.0 ** (-8.0 * (h + 1) / AH)) for h in range(AH)]


class _Stop(Exception):
    pass


def build_program(layers, first, nchunks, nch=2, debug=None, stop=None):
    NCH = nch
    NT = NCH * T
    ngroups = nchunks // NCH
    NL = len(layers)
    nc = bass.Bass("TRN2", target_bir_lowering=False)
    ssz = slab_sizes()
    plan = slab_plan()
    WTOT = sum(ssz)
    xin = nc.dram_tensor("xin", [nchunks * T, D], F32, kind="ExternalInput").ap()
    wst = [nc.dram_tensor(f"wst{i}", [128, WTOT], F32, kind="ExternalInput").ap() for i in range(NL)]
    ppd = nc.dram_tensor("pp", [128, PG_N + DEPTH * PL_N], F32, kind="ExternalInput").ap()
    cstd = nc.dram_tensor("cst", [128, C_N], F32, kind="ExternalInput").ap()
    yout = nc.dram_tensor("yout", [nchunks * T, D], F32, kind="ExternalOutput").ap()
    dbg_out = {}

    with contextlib.ExitStack() as st:
        P = Prog(nc)

        def sb(name, shape, dt=F32):
            return st.enter_context(nc.sbuf_tensor(name, shape, dt))

        pp = sb("pp_sb", [128, PG_N + DEPTH * PL_N]); b_pp = Buf("pp")
        cst = sb("cst_sb", [128, C_N]); b_cst = Buf("cst")
        identb = sb("identb", [128, 128], BF16)
        maskb = sb("maskb", [128, 256], BF16)
        mask0b = sb("mask0b", [128, 256], BF16)
        aneg = sb("aneg", [128, NL, NHS])
        hresT = sb("hresT", [128, KD, NT]); b_hres = Buf("hres")
        rT = sb("rT", [128, KD, NT]); b_rTj = [Buf(f"rT{j}") for j in range(KD)]
        onesdb = sb("onesdb", [128, 128], BF16)
        sqT = sb("sqT", [128, KD, NT]); b_sq = Buf("sq")
        hT = sb("hT", [128, KD, NT], BF16); b_hTj = [Buf(f"hT{j}") for j in range(KD)]
        Hst = [sb(f"Hst{i}", [128, NGR, 512]) for i in range(NL)]; b_Hst = [[Buf(f"Hst{i}_{g}") for g in range(NGR)] for i in range(NL)]
        Hbf = [sb(f"Hbf{i}", [128, NGR, 512], BF16) for i in range(NL)]; b_Hbf = [[Buf(f"Hbf{i}_{g}") for g in range(NGR)] for i in range(NL)]
        ctail = [sb(f"ctail{i}", [128, 24, 3]) for i in range(NL)]; b_ctail = [[Buf(f"ct{i}_{a}") for a in range(24)] for i in range(NL)]
        kT = [sb(f"kT{i}", [128, 2, T + NT], BF16) for i in range(NL)]; b_kT = [Buf(f"kT{i}") for i in range(NL)]
        vtok = [sb(f"vtok{i}", [128, NCH + 1, 128], BF16) for i in range(NL)]; b_vtok = [Buf(f"vt{i}") for i in range(NL)]
        wring = [sb(f"wring{i}", [128, SLAB], BF16) for i in range(NW)]; b_w = [Buf(f"w{i}") for i in range(NW)]
        qT = sb("qT", [128, 8, NT], BF16); b_qT = Buf("qT")
        xbcT = sb("xbcT", [128, 24, NT], BF16); b_xbcj = [Buf(f"xbc{j}") for j in range(24)]
        ust = [sb(f"ust{i}", [128, NT + 3]) for i in range(3)]; b_ust = [Buf(f"ust{i}") for i in range(3)]
        cacc = [sb(f"cacc{i}", [128, NT]) for i in range(3)]; b_cacc = [Buf(f"cacc{i}") for i in range(3)]
        zs = sb("zs", [128, NCH, DI], BF16); b_zs = Buf("zs")
        dtt = sb("dtt", [128, NCH, NHS]); dAt = sb("dAt", [128, NCH, NHS]); b_dt = Buf("dt")
        spt = sb("spt", [128, 4, NHS]); b_spt = Buf("spt")
        yT = sb("yT", [128, 16, NT], BF16); b_yT = Buf("yT")
        attT = sb("attT", [128, 8, NT], BF16); b_attT = Buf("attT")
        mixT = qT; b_mixT = b_qT
        hidT = xbcT; b_hid = b_xbcj
        gtmp = [sb(f"gtmp{i}", [128, NT]) for i in range(4)]; b_gtmp = [Buf(f"gtmp{i}") for i in range(4)]
        lnt = sb("lnt", [128, 2, NT]); b_lnt = Buf("lnt")
        xld = [sb(f"xld{i}", [128, D]) for i in range(NCH)]; b_xld = [Buf(f"xld{i}") for i in range(NCH)]
        xst = [sb(f"xst{i}", [128, 8]) for i in range(NCH)]; b_xst = [Buf(f"xst{i}") for i in range(NCH)]
        osb = [sqT[:, :, :].rearrange("p k t -> p (k t)")[:, D:2 * D]]; b_osb = [Buf("osb")]
        TM = []
        NTM = 1
        for i in range(NTM):
            d = dict(
                sm=sb(f"sm{i}", [128, 5, NHS]),
                X=sb(f"X{i}", [128, DI], BF16), Xd=sb(f"Xd{i}", [128, DI], BF16), xs=sb(f"xs{i}", [128, DI], BF16),
                Dx=sb(f"Dx{i}", [128, DI], BF16),
                Btok=sb(f"Btok{i}", [128, 512], BF16), CBm=sb(f"CBm{i}", [128, NGR, 128], BF16),
                Rb=[sb(f"Rb{i}_{j}", [128, 4, 128]) for j in range(3)],
                LT=[sb(f"LT{i}_{j}", [128, 4, 128], BF16) for j in range(3)],
                MT=[sb(f"MT{i}_{j}", [128, 4, 128], BF16) for j in range(3)],
                t1=[sb(f"t1_{i}_{j}", [128, 512]) for j in range(2)], y=sb(f"y{i}", [128, DI]),
                ssq=sb(f"ssq{i}", [128, 12]), yn=sb(f"yn{i}", [128, DI], BF16),
                sc=[sb(f"sc{i}_{j}", [128, 256]) for j in range(3)],
                pp_=[sb(f"p{i}_{j}", [128, 256], BF16) for j in range(5)],
                pT=[sb(f"pT{i}_{j}", [128, 256], BF16) for j in range(5)],
                ast=sb(f"ast{i}", [128, 6, AH]),
                att=sb(f"att{i}", [128, D], BF16),
            )
            d["b"] = {k: Buf(f"{k}{i}") for k in ("sm", "X", "Xd", "xs", "Dx", "Btok", "CBm", "t1", "y", "junk", "ssq",
                                                    "yn", "ast", "att")}
            d["b"]["asth"] = [Buf(f"asth{i}_{j}") for j in range(AH)]
            d["b"]["yg"] = [Buf(f"yg{i}_{j}") for j in range(NGR)]
            d["b"]["t1"] = [Buf(f"t1_{i}_{j}") for j in range(2)]
            for k in ("Rb", "LT", "MT", "sc", "pp_", "pT"):
                d["b"][k] = [Buf(f"{k}{i}_{j}") for j in range(5)]
            TM.append(d)

        pgA = st.enter_context(nc.psum_tensor("pgA", [128, 1024], F32))
        pgB = st.enter_context(nc.psum_tensor("pgB", [128, 1024], F32))
        pD = [st.enter_context(nc.psum_tensor(f"pD{i}", [128, 512], F32)) for i in range(2)]
        pBt = [st.enter_context(nc.psum_tensor(f"pBt{i}", [128, 1024], BF16)) for i in range(2)]
        b_pD = [Buf(f"pD{i}", True) for i in range(2)]
        b_pg = [Buf(f"pg{i}", True) for i in range(4)] + b_pD
        b_pB = [Buf(f"pB{i}", True) for i in range(2)]

        def pg(i):
            if i >= 4:
                return pD[i - 4][:, :]
            t = pgA if i < 2 else pgB
            o = (i % 2) * 512
            return t[:, o:o + 512]

        def cc(a):
            return cst[:, a:a + 128]

        def pcol(l, off, n=1, j=0):
            o = PG_N + l * PL_N + off + j
            return pp[:, o:o + n]

        out_bufs = []

        P.op("sp", lambda e: e.dma_start(out=pp[:], in_=ppd), writes=[b_pp], dma=True)
        P.op("sp", lambda e: e.dma_start(out=cst[:], in_=cstd), writes=[b_cst], dma=True)
        b_setup = Buf("setup")
        P.op("dve", lambda e: e.tensor_copy(out=identb[:], in_=cst[:, C_ID:C_ID + 128]), reads=[b_cst], writes=[b_setup])
        P.op("dve", lambda e: e.tensor_copy(out=onesdb[:], in_=cst[:, C_ONESD:C_ONESD + 128]), reads=[b_cst], writes=[b_setup])
        P.op("dve", lambda e: e.tensor_copy(out=maskb[:], in_=cst[:, C_M:C_M + 256]), reads=[b_cst], writes=[b_setup])
        P.op("dve", lambda e: e.tensor_copy(out=mask0b[:], in_=cst[:, C_M0:C_M0 + 256]), reads=[b_cst], writes=[b_setup])
        for i, l in enumerate(layers):
            P.op("act", lambda e, i=i, l=l: e.activation(out=aneg[:, i, :], in_=pcol(l, PL_ALOG, NHS), func=AF.Exp),
                 reads=[b_pp], writes=[b_setup])
            P.op("dve", lambda e, i=i: e.tensor_scalar(out=aneg[:, i, :], in0=aneg[:, i, :], scalar1=-1.0, scalar2=None,
                                                       op0=ALU.mult), reads=[b_setup], writes=[b_setup])
            P.op("dve", lambda e, i=i: e.memset(Hst[i][:], 0.0), writes=b_Hst[i])
            P.op("dve", lambda e, i=i: e.memset(Hbf[i][:], 0.0), writes=b_Hbf[i])
            P.op("dve", lambda e, i=i: e.memset(ctail[i][:], 0.0), writes=b_ctail[i])
            P.op("dve", lambda e, i=i: e.memset(kT[i][:], 0.0), writes=[b_kT[i]])
            P.op("dve", lambda e, i=i: e.memset(vtok[i][:], 0.0), writes=[b_vtok[i]])

        wstate = {"n": 0}
        nslab = len(plan)
        seq_slabs = []
        for g in range(ngroups):
            for i in range(NL):
                for s in range(nslab):
                    seq_slabs.append((i, s))
        soff = np.concatenate([[0], np.cumsum(ssz)]).astype(int)

        wbf = [nc.dram_tensor(f"wbf{i}", [128, WTOT], BF16, kind="Internal").ap() for i in range(NL)]
        b_wbf = [[Buf(f"wbf{i}_{s}") for s in range(nslab)] for i in range(NL)]
        conv_order = [(i, s) for i in range(NL) for s in range(nslab)]
        cvn = {"n": 0}

        def conv_more(k):
            for _ in range(k):
                if cvn["n"] >= len(conv_order):
                    return
                i, s = conv_order[cvn["n"]]
                cvn["n"] += 1
                P.op("pool", lambda e, i=i, s=s: e.dma_start(
                    out=wbf[i][:, int(soff[s]):int(soff[s + 1])], in_=wst[i][:, int(soff[s]):int(soff[s + 1])],
                    max_dma_last_dim=8192), writes=[b_wbf[i][s]], dma=True)

        def issue_slab(n):
            if n >= len(seq_slabs):
                return
            i, s = seq_slabs[n]
            slot = n % NW
            P.op("sp", lambda e, i=i, s=s, slot=slot: e.dma_start(
                out=wring[slot][:, 0:ssz[s]], in_=wbf[i][:, int(soff[s]):int(soff[s + 1])]),
                reads=[b_wbf[i][s]], writes=[b_w[slot]], dma=True)

        def start_stream():
            conv_more(NW + 5)
            for n in range(NW):
                issue_slab(n)

        class SlabCursor:
            def __init__(self):
                self.n = -1

            def next(self):
                self.n += 1
                return self.n % NW

            def done(self):
                conv_more(1)
                issue_slab(self.n + NW)
        cur = SlabCursor()

        def ckpt(n):
            if stop is not None and stop == n:
                P.halt = True

        def dump(name, ap, bufs, shape, dt=F32):
            if debug is None or name not in debug:
                return
            t = nc.dram_tensor("dbg_" + name, shape, dt, kind="ExternalOutput").ap()
            bo = Buf("dbgo_" + name)
            P.op("sp", lambda e: e.dma_start(out=t, in_=ap), reads=bufs, writes=[bo], dma=True)
            out_bufs.append(bo)
            dbg_out[name] = True

        def ln_pre(j):
            P.op("act", lambda e, j=j: e.activation(out=hT[:, j, :], in_=rT[:, j, :], func=AF.Copy),
                 reads=[b_rTj[j]], writes=[b_hTj[j]])

        def ln_feature(l, og, ob):
            bk0, bk1 = nextbank(), nextbank()
            for j in range(KD):
                P.op("pe", lambda e, j=j: e.matmul(out=pg(bk0)[:, 0:NT], lhsT=onesdb[:], rhs=hT[:, j, :], start=(j == 0),
                                                    stop=(j == KD - 1)), reads=[b_hTj[j], b_setup], writes=[b_pg[bk0]])
            for j in range(KD):
                P.op("dve", lambda e, j=j: e.tensor_tensor(out=rT[:, j, :], in0=rT[:, j, :], in1=pg(bk0)[:, 0:NT],
                                                           op=ALU.subtract), reads=[b_rTj[j], b_pg[bk0]], writes=[b_rTj[j]])
                P.op("act", lambda e, j=j: e.activation(out=hT[:, j, :], in_=rT[:, j, :], func=AF.Square),
                     reads=[b_rTj[j]], writes=[b_hTj[j]])
                P.op("pe", lambda e, j=j: e.matmul(out=pg(bk1)[:, 0:NT], lhsT=onesdb[:], rhs=hT[:, j, :], start=(j == 0),
                                                    stop=(j == KD - 1)), reads=[b_hTj[j], b_setup], writes=[b_pg[bk1]])
            P.op("dve", lambda e: e.tensor_scalar(out=lnt[:, 0, :], in0=pg(bk1)[:, 0:NT], scalar1=LN_EPS, scalar2=None,
                                                  op0=ALU.add), reads=[b_pg[bk1]], writes=[b_lnt])
            P.op("act", lambda e: e.activation(out=lnt[:, 0, :], in_=lnt[:, 0, :], func=AF.Sqrt),
                 reads=[b_lnt], writes=[b_lnt])
            P.op("dve", lambda e: e.reciprocal(out=lnt[:, 1, :], in_=lnt[:, 0, :]), reads=[b_lnt], writes=[b_lnt])
            for j in range(KD):
                P.op("dve", lambda e, j=j: e.tensor_tensor(out=rT[:, j, :], in0=rT[:, j, :], in1=lnt[:, 1, :], op=ALU.mult),
                     reads=[b_rTj[j], b_lnt], writes=[b_rTj[j]])
                P.op("act", lambda e, j=j: e.activation(out=hresT[:, j, :], in_=rT[:, j, :], func=AF.Identity,
                                                         scale=pcol(l, og, 1, j), bias=pcol(l, ob, 1, j)),
                     reads=[b_rTj[j], b_pp], writes=[b_hres])
                P.op("dve", lambda e, j=j: e.tensor_scalar(out=hT[:, j, :], in0=rT[:, j, :], scalar1=pcol(l, og, 1, j),
                                                            scalar2=pcol(l, ob, 1, j), op0=ALU.mult, op1=ALU.add),
                     reads=[b_rTj[j], b_pp], writes=[b_hTj[j]])

        def gemm_b(slot, woff, kc, actT, b_act, bank):
            for k in range(kc):
                ba = b_act[k] if isinstance(b_act, list) else b_act
                P.op("pe", lambda e, k=k: e.matmul(out=pg(bank)[:, 0:NT], lhsT=wring[slot][:, woff + k * 128:woff + (k + 1) * 128],
                                                    rhs=actT[:, k, :], start=(k == 0), stop=(k == kc - 1)),
                     reads=[b_w[slot], ba], writes=[b_pg[bank]])

        gbank = {"n": 0}

        def nextbank():
            gbank["n"] = (gbank["n"] + 1) % 6
            return gbank["n"]

        def p0a(g):
            for c in range(NCH):
                gc = g * NCH + c
                xb = c
                xt, bx, sx, bsx = xld[xb], b_xld[xb], xst[xb], b_xst[xb]
                P.op("pool", lambda e, xt=xt, gc=gc: e.dma_start(out=xt[:], in_=xin[gc * T:(gc + 1) * T, :]),
                     writes=[bx], dma=True)
                if first:
                    P.op("act", lambda e, xt=xt, sx=sx: e.activation(out=sqT[:, :, :].rearrange("p k t -> p (k t)")[:, 0:D], in_=xt[:], func=AF.Identity,
                                                                      accum_out=sx[:, 0:1]),
                         reads=[bx], writes=[b_sq, bsx])
                    P.op("dve", lambda e, sx=sx: e.tensor_scalar(out=sx[:, 1:2], in0=sx[:, 0:1], scalar1=-1.0 / D,
                                                                 scalar2=None, op0=ALU.mult), reads=[bsx], writes=[bsx])
                    P.op("dve", lambda e, xt=xt, sx=sx: e.tensor_scalar(out=xt[:], in0=xt[:], scalar1=sx[:, 1:2],
                                                                        scalar2=None, op0=ALU.add),
                         reads=[bx, bsx], writes=[bx])
                    P.op("act", lambda e, xt=xt, sx=sx: e.activation(out=sqT[:, :, :].rearrange("p k t -> p (k t)")[:, 0:D], in_=xt[:], func=AF.Square,
                                                                      accum_out=sx[:, 2:3]),
                         reads=[bx], writes=[b_sq, bsx])
                    P.op("dve", lambda e, sx=sx: e.tensor_scalar(out=sx[:, 3:4], in0=sx[:, 2:3], scalar1=1.0 / D,
                                                                 scalar2=LN_EPS, op0=ALU.mult, op1=ALU.add),
                         reads=[bsx], writes=[bsx])
                    P.op("act", lambda e, sx=sx: e.activation(out=sx[:, 4:5], in_=sx[:, 3:4], func=AF.Sqrt),
                         reads=[bsx], writes=[bsx])
                    P.op("dve", lambda e, sx=sx: e.reciprocal(out=sx[:, 5:6], in_=sx[:, 4:5]), reads=[bsx], writes=[bsx])
                    P.op("dve", lambda e, xt=xt, sx=sx: e.tensor_scalar(out=xt[:], in0=xt[:], scalar1=sx[:, 5:6],
                                                                        scalar2=None, op0=ALU.mult),
                         reads=[bx, bsx], writes=[bx])

        def p0b(g):
            for c in range(NCH):
                gc = g * NCH + c
                xt, bx = xld[c], b_xld[c]
                for j in range(KD):
                    P.op("pe", lambda e, xt=xt, j=j: e.transpose(out=pgA[:, j * 128:(j + 1) * 128],
                                                                 in_=xt[:, j * 128:(j + 1) * 128], identity=cc(C_ID)),
                         reads=[bx, b_cst], writes=[b_pg[j // 4]])
                for j in range(KD):
                    src = pgA[:, j * 128:(j + 1) * 128]
                    if first:
                        P.op("act", lambda e, j=j, c=c, src=src: e.activation(
                            out=hresT[:, j, c * T:(c + 1) * T], in_=src, func=AF.Identity,
                            scale=pp[:, j:j + 1], bias=pp[:, 8 + j:9 + j]),
                            reads=[b_pg[j // 4], b_pp], writes=[b_hres])
                        P.op("dve", lambda e, j=j, c=c, src=src: e.tensor_scalar(
                            out=hT[:, j, c * T:(c + 1) * T], in0=src, scalar1=pp[:, j:j + 1], scalar2=pp[:, 8 + j:9 + j],
                            op0=ALU.mult, op1=ALU.add), reads=[b_pg[j // 4], b_pp], writes=[b_hTj[j]])
                    else:
                        P.op("act", lambda e, j=j, c=c, src=src: e.activation(
                            out=hresT[:, j, c * T:(c + 1) * T], in_=src, func=AF.Identity),
                            reads=[b_pg[j // 4]], writes=[b_hres])
                        P.op("dve", lambda e, j=j, c=c, src=src: e.tensor_copy(out=hT[:, j, c * T:(c + 1) * T], in_=src),
                             reads=[b_pg[j // 4]], writes=[b_hTj[j]])

        for g in range(ngroups):
            if g == 0:
                p0a(0)
                start_stream()
            p0b(g)
            if g == 0:
                dump("h0T", hresT[:], [b_hres], [128, KD, NT])
                ckpt(0)

            for li, l in enumerate(layers):
                nblk = 0
                pendB, pendC = [], []
                for s in range(9):
                    slot = cur.next()
                    for bi, (t, a, ncols, kc) in enumerate(plan[s]):
                        bank = nextbank()
                        gemm_b(slot, bi * 1024, 8, hT, b_hTj, bank)
                        src = pg(bank)[:, 0:NT]
                        if t == "q":
                            P.op("act", lambda e, a=a, src=src: e.activation(out=qT[:, a, :], in_=src, func=AF.Copy,
                                                                              scale=0.125),
                                 reads=[b_pg[bank]], writes=[b_qT])
                        elif t == "kd":
                            P.op("dve", lambda e, a=a, src=src: e.tensor_copy(out=kT[li][:, a, T:T + NT], in_=src),
                                 reads=[b_pg[bank]], writes=[b_kT[li]])
                        else:
                            ub = nblk % 3
                            nblk += 1
                            u, bu, ac, bac = ust[ub], b_ust[ub], cacc[ub], b_cacc[ub]
                            P.op("act", lambda e, u=u, src=src: e.activation(out=u[:, 3:3 + NT], in_=src, func=AF.Copy),
                                 reads=[b_pg[bank]], writes=[bu])
                            P.op("dve", lambda e, u=u, a=a: e.tensor_copy(out=u[:, 0:3], in_=ctail[li][:, a, :]),
                                 reads=[b_ctail[li][a]], writes=[bu])

                            def stB(u=u, bu=bu, ac=ac, bac=bac, a=a):
                                P.op("pool", lambda e: e.tensor_scalar(
                                    out=ac[:], in0=u[:, 0:NT], scalar1=pcol(l, PL_CONVW, 1, a * 4 + 0),
                                    scalar2=pcol(l, PL_CONVB, 1, a), op0=ALU.mult, op1=ALU.add),
                                    reads=[bu, b_pp], writes=[bac])
                                for k in (1, 2, 3):
                                    P.op("dve", lambda e, k=k: e.scalar_tensor_tensor(
                                        out=ac[:], in0=u[:, k:k + NT], scalar=pcol(l, PL_CONVW, 1, a * 4 + k), in1=ac[:],
                                        op0=ALU.mult, op1=ALU.add), reads=[bu, bac, b_pp], writes=[bac])
                                P.op("dve", lambda e: e.tensor_copy(out=ctail[li][:, a, :], in_=u[:, NT:NT + 3]),
                                     reads=[bu], writes=[b_ctail[li][a]])

                            def stC(ac=ac, bac=bac, a=a):
                                P.op("act", lambda e: e.activation(out=xbcT[:, a, :], in_=ac[:], func=AF.Silu),
                                     reads=[bac], writes=[b_xbcj[a]])
                            if len(pendB) >= 1:
                                fb = pendB.pop(0)
                                fb[0]()
                                pendC.append(fb[1])
                            if len(pendC) >= 2:
                                pendC.pop(0)()
                            pendB.append((stB, stC))
                    cur.done()

                for s in range(4):
                    slot = cur.next()
                    for c in range(NCH):
                        bank = nextbank()
                        for k in range(KD):
                            P.op("pe", lambda e, k=k, c=c, slot=slot, bank=bank: e.matmul(
                                out=pg(bank), lhsT=hT[:, k, c * T:(c + 1) * T], rhs=wring[slot][:, k * 512:(k + 1) * 512],
                                start=(k == 0), stop=(k == KD - 1)), reads=[b_w[slot], b_hTj[k]], writes=[b_pg[bank]])
                        P.op("act", lambda e, c=c, s=s, bank=bank: e.activation(
                            out=zs[:, c, s * 512:(s + 1) * 512], in_=pg(bank), func=AF.Silu),
                            reads=[b_pg[bank]], writes=[b_zs])
                    cur.done()
                slot = cur.next()
                for c in range(NCH):
                    bank = nextbank()
                    for k in range(KD):
                        P.op("pe", lambda e, k=k, c=c, slot=slot, bank=bank: e.matmul(
                            out=pg(bank)[:, 0:160], lhsT=hT[:, k, c * T:(c + 1) * T],
                            rhs=wring[slot][:, k * 160:(k + 1) * 160], start=(k == 0), stop=(k == KD - 1)),
                            reads=[b_w[slot], b_hTj[k]], writes=[b_pg[bank]])
                    P.op("act", lambda e, c=c, bank=bank: e.activation(out=vtok[li][:, c + 1, :], in_=pg(bank)[:, 0:128],
                                                                        func=AF.Copy),
                         reads=[b_pg[bank]], writes=[b_vtok[li]])
                    P.op("dve", lambda e, bank=bank: e.tensor_tensor(out=spt[:, 0, :], in0=pg(bank)[:, 128:160],
                                                                     in1=pcol(l, PL_DTB, NHS), op=ALU.add),
                         reads=[b_pg[bank], b_pp], writes=[b_spt])
                    P.op("act", lambda e: e.activation(out=spt[:, 1, :], in_=spt[:, 0, :], func=AF.Abs),
                         reads=[b_spt], writes=[b_spt])
                    P.op("act", lambda e: e.activation(out=spt[:, 2, :], in_=spt[:, 1, :], func=AF.Exp, scale=-1.0),
                         reads=[b_spt], writes=[b_spt])
                    P.op("act", lambda e: e.activation(out=spt[:, 3, :], in_=spt[:, 2, :], func=AF.Ln, bias=1.0),
                         reads=[b_spt], writes=[b_spt])
                    P.op("dve", lambda e, c=c: e.scalar_tensor_tensor(out=dtt[:, c, :], in0=spt[:, 0, :], scalar=0.0,
                                                                      in1=spt[:, 3, :], op0=ALU.max, op1=ALU.add),
                         reads=[b_spt], writes=[b_dt])
                    P.op("dve", lambda e, c=c: e.tensor_tensor(out=dAt[:, c, :], in0=dtt[:, c, :], in1=aneg[:, li, :],
                                                               op=ALU.mult), reads=[b_dt, b_setup], writes=[b_dt])
                cur.done()
                for fb in pendB:
                    fb[0]()
                    pendC.append(fb[1])
                for fc in pendC:
                    fc()
                if g == 0 and li == 0:
                    dump("qT", qT[:], [b_qT], [128, 8, NT], BF16)
                    dump("kT", kT[li][:], [b_kT[li]], [128, 2, T + NT], BF16)
                    dump("xbcT", xbcT[:], b_xbcj, [128, 24, NT], BF16)
                    ckpt(1)
                if g == 0 and li == 0:
                    dump("zs", zs[:], [b_zs], [128, NCH, DI], BF16)
                    dump("dtt", dtt[:], [b_dt], [128, NCH, NHS])
                    dump("vtok", vtok[li][:], [b_vtok[li]], [128, NCH + 1, 128], BF16)
                    ckpt(2)

                def chunk_gen(c):
                        gc = g * NCH + c
                        tm = TM[gc % NTM]
                        tb = tm["b"]
                        cs = slice(c * T, (c + 1) * T)
                        sm = tm["sm"]
                        P.op("pe", lambda e, c=c: e.matmul(out=pg(0)[:, 0:32], lhsT=cc(C_U), rhs=dAt[:, c, :], start=True,
                                                           stop=True), reads=[b_dt, b_cst], writes=[b_pg[0]])
                        P.op("pe", lambda e, c=c: e.matmul(out=pg(0)[:, 32:64], lhsT=cc(C_ONES), rhs=dAt[:, c, :], start=True,
                                                           stop=True), reads=[b_dt, b_cst], writes=[b_pg[0]])
                        P.op("dve", lambda e, sm=sm: e.tensor_copy(out=sm[:, 0, :], in_=pg(0)[:, 0:32]),
                             reads=[b_pg[0]], writes=[tb["sm"]])
                        P.op("dve", lambda e, sm=sm: e.scalar_tensor_tensor(out=sm[:, 1, :], in0=sm[:, 0, :], scalar=-1.0,
                                                                            in1=pg(0)[:, 32:64], op0=ALU.mult, op1=ALU.add),
                             reads=[b_pg[0], tb["sm"]], writes=[tb["sm"]])
                        P.op("act", lambda e, sm=sm: e.activation(out=sm[:, 2, :], in_=sm[:, 0, :], func=AF.Exp),
                             reads=[tb["sm"]], writes=[tb["sm"]])
                        P.op("act", lambda e, sm=sm: e.activation(out=sm[:, 3, :], in_=sm[:, 1, :], func=AF.Exp),
                             reads=[tb["sm"]], writes=[tb["sm"]])
                        P.op("act", lambda e, sm=sm: e.activation(out=sm[:, 4, :], in_=pg(0)[:, 32:64], func=AF.Exp),
                             reads=[b_pg[0], tb["sm"]], writes=[tb["sm"]])
                        for cb in range(16):
                            P.op("pe", lambda e, cb=cb, cs=cs: e.transpose(
                                out=pBt[cb // 8][:, (cb % 8) * 128:(cb % 8 + 1) * 128], in_=xbcT[:, cb, cs], identity=identb[:]),
                                reads=[b_xbcj[cb], b_setup], writes=[b_pB[cb // 8]])
                        for hf in range(2):
                            src = pBt[hf][:, :].rearrange("p (h d) -> p h d", d=HP)
                            P.op("dve", lambda e, tm=tm, hf=hf, src=src, c=c: e.tensor_tensor(
                                out=tm["X"][:, hf * 1024:(hf + 1) * 1024].rearrange("p (h d) -> p h d", d=HP), in0=src,
                                in1=dtt[:, c, hf * 16:(hf + 1) * 16].unsqueeze(2).to_broadcast([128, 16, HP]), op=ALU.mult),
                                reads=[b_pB[hf], b_dt], writes=[tb["X"]])
                            P.op("act", lambda e, tm=tm, hf=hf: e.activation(out=tm["xs"][:, hf * 1024:(hf + 1) * 1024],
                                                                              in_=pBt[hf][:, :], func=AF.Copy),
                                 reads=[b_pB[hf]], writes=[tb["xs"]])
                        P.op("dve", lambda e, tm=tm, sm=sm: e.tensor_tensor(
                            out=tm["Xd"][:, :].rearrange("p (h d) -> p h d", d=HP),
                            in0=tm["X"][:, :].rearrange("p (h d) -> p h d", d=HP),
                            in1=sm[:, 3, :].unsqueeze(2).to_broadcast([128, NHS, HP]), op=ALU.mult),
                            reads=[tb["X"], tb["sm"]], writes=[tb["Xd"]])
                        for gg in range(NGR):
                            P.op("pe", lambda e, gg=gg, cs=cs: e.matmul(out=pD[0][:, gg * 128:(gg + 1) * 128],
                                                                         lhsT=xbcT[:, 16 + gg, cs], rhs=xbcT[:, 20 + gg, cs],
                                                                         start=True, stop=True),
                                 reads=[b_xbcj[16 + gg], b_xbcj[20 + gg]], writes=[b_pD[0]])
                        P.op("dve", lambda e, tm=tm: e.tensor_tensor(
                            out=tm["CBm"][:], in0=pD[0][:, :].rearrange("p (g l) -> p g l", l=128),
                            in1=cc(C_U).unsqueeze(1).to_broadcast([128, NGR, 128]), op=ALU.mult),
                            reads=[b_pD[0], b_cst], writes=[tb["CBm"]])
                        def s1(b, tm=tm, tb=tb, c=c):
                            jb = b % 3
                            Rb, LT = tm["Rb"][jb], tm["LT"][jb]
                            bRb, bLT = tb["Rb"][jb], tb["LT"][jb]
                            P.op("pool" if b % 2 == 0 else "dve", lambda e: e.tensor_tensor(
                                out=Rb[:], in0=cc(C_U).unsqueeze(1).to_broadcast([128, 4, 128]),
                                in1=dAt[:, c, 4 * b:4 * b + 4].unsqueeze(2).to_broadcast([128, 4, 128]), op=ALU.mult),
                                reads=[b_cst, b_dt], writes=[bRb])
                            pdj = (b + 1) % 2
                            P.op("pe", lambda e: e.matmul(out=pD[pdj][:, :], lhsT=cc(C_L),
                                                          rhs=Rb[:, :, :].rearrange("p h l -> p (h l)"), start=True, stop=True),
                                 reads=[bRb, b_cst], writes=[b_pD[pdj]])
                            P.op("act", lambda e: e.activation(out=LT[:, :, :].rearrange("p h l -> p (h l)"), in_=pD[pdj][:, :],
                                                               func=AF.Exp), reads=[b_pD[pdj]], writes=[bLT])

                        def s2(b, tm=tm, tb=tb):
                            jb = b % 3
                            gg = b // 2
                            LT, MT = tm["LT"][jb], tm["MT"][jb]
                            bLT, bMT = tb["LT"][jb], tb["MT"][jb]
                            P.op("dve", lambda e: e.tensor_tensor(
                                out=MT[:], in0=LT[:], in1=tm["CBm"][:, gg, :].unsqueeze(1).to_broadcast([128, 4, 128]),
                                op=ALU.mult), reads=[bLT, tb["CBm"]], writes=[bMT])
                            yb = gg % 2
                            for hb in range(4):
                                h = 4 * b + hb
                                P.op("pe", lambda e, hb=hb, h=h: e.matmul(
                                    out=pg(yb)[:, (h % 8) * HP:(h % 8 + 1) * HP], lhsT=MT[:, hb, :],
                                    rhs=tm["X"][:, h * HP:(h + 1) * HP], start=True, stop=True),
                                    reads=[bMT, tb["X"]], writes=[b_pg[yb]])

                        def ybank(gg):
                            if gg % 2 == 0:
                                return pg(2), b_pg[2], pg(3), b_pg[3]
                            return pBt[0][:, :].bitcast(F32), b_pB[0], pBt[1][:, :].bitcast(F32), b_pB[1]

                        def gpe(gg, tm=tm, tb=tb, cs=cs):
                            po, bpo, pst, bpst = ybank(gg)
                            P.op("pe", lambda e: e.matmul(out=po, lhsT=xbcT[:, 20 + gg, cs], rhs=Hbf[li][:, gg, :],
                                                          start=True, stop=True),
                                 reads=[b_xbcj[20 + gg], b_Hbf[li][gg]], writes=[bpo])
                            P.op("pe", lambda e: e.matmul(out=pst, lhsT=tm["Btok"][:, gg * 128:(gg + 1) * 128],
                                                          rhs=tm["Xd"][:, gg * 512:(gg + 1) * 512], start=True, stop=True),
                                 reads=[tb["Btok"], tb["Xd"]], writes=[bpst])

                        def comb(gg, tm=tm, tb=tb, sm=sm, c=c):
                            yb = gg % 2
                            po, bpo, pst, bpst = ybank(gg)
                            gsl = slice(gg * 512, (gg + 1) * 512)
                            P.op("act", lambda e: e.activation(out=tm["t1"][yb][:], in_=pg(yb), func=AF.Copy),
                                 reads=[b_pg[yb]], writes=[tb["t1"][yb]])
                            P.op("dve", lambda e: e.tensor_tensor(
                                out=tm["y"][:, gsl].rearrange("p (h d) -> p h d", d=HP),
                                in0=po.rearrange("p (h d) -> p h d", d=HP),
                                in1=sm[:, 2, gg * 8:(gg + 1) * 8].unsqueeze(2).to_broadcast([128, 8, HP]), op=ALU.mult),
                                reads=[bpo, tb["sm"]], writes=[tb["yg"][gg]])
                            P.op("dve", lambda e: e.tensor_tensor(out=tm["y"][:, gsl], in0=tm["y"][:, gsl], in1=tm["t1"][yb][:],
                                                                  op=ALU.add), reads=[tb["yg"][gg], tb["t1"][yb]], writes=[tb["yg"][gg]])
                            P.op("dve", lambda e: e.tensor_tensor(out=tm["y"][:, gsl], in0=tm["y"][:, gsl],
                                                                  in1=zs[:, c, gsl], op=ALU.mult),
                                 reads=[tb["yg"][gg], b_zs], writes=[tb["yg"][gg]])
                            P.op("dve", lambda e: e.tensor_tensor(out=tm["y"][:, gsl], in0=tm["y"][:, gsl], in1=tm["Dx"][:, gsl],
                                                                  op=ALU.add), reads=[tb["yg"][gg], tb["Dx"]], writes=[tb["yg"][gg]])
                            P.op("act", lambda e: e.activation(out=tm["t1"][yb][:], in_=tm["y"][:, gsl], func=AF.Square,
                                                               accum_out=tm["ssq"][:, gg:gg + 1]),
                                 reads=[tb["yg"][gg]], writes=[tb["t1"][yb], tb["ssq"]])
                            P.op("dve", lambda e: e.tensor_tensor(
                                out=Hst[li][:, gg, :].rearrange("p (h d) -> p h d", d=HP),
                                in0=Hst[li][:, gg, :].rearrange("p (h d) -> p h d", d=HP),
                                in1=sm[:, 4, gg * 8:(gg + 1) * 8].unsqueeze(2).to_broadcast([128, 8, HP]), op=ALU.mult),
                                reads=[b_Hst[li][gg], tb["sm"]], writes=[b_Hst[li][gg]])
                            P.op("dve", lambda e: e.tensor_tensor(out=Hst[li][:, gg, :], in0=Hst[li][:, gg, :], in1=pst,
                                                                  op=ALU.add),
                                 reads=[bpst], writes=[b_Hst[li][gg]])
                            P.op("act", lambda e: e.activation(out=Hbf[li][:, gg, :], in_=Hst[li][:, gg, :], func=AF.Copy),
                                 reads=[b_Hst[li][gg]], writes=[b_Hbf[li][gg]])

                        s1(0)
                        s1(1)
                        for gg in range(NGR):
                            P.op("pe", lambda e, gg=gg, cs=cs: e.transpose(out=pBt[0][:, gg * 128:(gg + 1) * 128],
                                                                            in_=xbcT[:, 16 + gg, cs], identity=identb[:]),
                                 reads=[b_xbcj[16 + gg], b_setup], writes=[b_pB[0]])
                        P.op("act", lambda e, tm=tm: e.activation(out=tm["Btok"][:], in_=pBt[0][:, 0:512], func=AF.Copy),
                             reads=[b_pB[0]], writes=[tb["Btok"]])
                        def dx_ops(tm=tm, tb=tb, c=c):
                            P.op("dve", lambda e: e.tensor_tensor(
                                out=tm["Dx"][:, :].rearrange("p (h d) -> p h d", d=HP),
                                in0=tm["xs"][:, :].rearrange("p (h d) -> p h d", d=HP),
                                in1=pcol(l, PL_DSK, NHS).unsqueeze(2).to_broadcast([128, NHS, HP]), op=ALU.mult),
                                reads=[tb["xs"], b_pp], writes=[tb["Dx"]])
                            P.op("dve", lambda e: e.tensor_tensor(out=tm["Dx"][:], in0=tm["Dx"][:], in1=zs[:, c, :], op=ALU.mult),
                                 reads=[tb["Dx"], b_zs], writes=[tb["Dx"]])
                        yield
                        pend_g, pend_c = [], []
                        for t in range(2, 10):
                            if t < 8:
                                s1(t)
                            s2(t - 2)
                            if t == 3:
                                dx_ops()
                            for gg in pend_c:
                                comb(gg)
                            pend_c = []
                            for gg in pend_g:
                                gpe(gg)
                                pend_c.append(gg)
                            pend_g = []
                            if (t - 2) % 2 == 1:
                                pend_g.append((t - 2) // 2)
                        for gg in pend_c:
                            comb(gg)
                        for gg in pend_g:
                            gpe(gg)
                            comb(gg)

                        ast = tm["ast"]
                        mk = mask0b if gc == 0 else maskb

                        def a1(h, tm=tm, tb=tb, ast=ast, mk=mk, cs=cs, c=c):
                            hh, j, kv = h % 2, h // 2, h // 8
                            x = h % 2
                            sc, bsc = tm["sc"][h % 3], tb["sc"][h % 3]
                            pp_, bpp = tm["pp_"][h % 5], tb["pp_"][h % 5]
                            P.op("pe", lambda e: e.matmul(
                                out=pD[x][:, 0:256], lhsT=qT[hh * 64:(hh + 1) * 64, j, cs],
                                rhs=kT[li][hh * 64:(hh + 1) * 64, kv, c * T:c * T + 256], start=True, stop=False),
                                reads=[b_qT, b_kT[li]], writes=[b_pD[x]])
                            P.op("pe", lambda e: e.matmul(out=pD[x][:, 0:256], lhsT=identb[:], rhs=mk[:], start=False, stop=True),
                                 reads=[b_setup], writes=[b_pD[x]])
                            P.op("dve", lambda e: e.scalar_tensor_tensor(
                                out=sc[:], in0=cst[:, C_R0:C_R0 + 256], scalar=SLOPES[h], in1=pD[x][:, 0:256], op0=ALU.mult,
                                op1=ALU.add), reads=[b_cst, b_pD[x]], writes=[bsc])
                            P.op("dve", lambda e: e.reduce_max(out=ast[:, 0, h:h + 1], in_=sc[:], axis=AX.X),
                                 reads=[bsc], writes=[tb["asth"][h]])
                            P.op("dve", lambda e: e.tensor_scalar(
                                out=ast[:, 1, h:h + 1], in0=ast[:, 0, h:h + 1], scalar1=pcol(l, PL_SINK, 1, h), scalar2=-1.0,
                                op0=ALU.max, op1=ALU.mult), reads=[tb["asth"][h], b_pp], writes=[tb["asth"][h]])
                            P.op("act", lambda e: e.activation(
                                out=pp_[:], in_=sc[:], func=AF.Exp, bias=ast[:, 1, h:h + 1], accum_out=ast[:, 2, h:h + 1]),
                                reads=[bsc, tb["asth"][h]], writes=[bpp, tb["asth"][h]])

                        def a2(h, tm=tm, tb=tb):
                            x = h % 2
                            r = h % 5
                            pp_, bpp = tm["pp_"][r], tb["pp_"][r]
                            pT, bpT = tm["pT"][r], tb["pT"][r]
                            for half in range(2):
                                P.op("pe", lambda e, half=half: e.transpose(
                                    out=pBt[x][:, half * 128:(half + 1) * 128], in_=pp_[:, half * 128:(half + 1) * 128],
                                    identity=identb[:]), reads=[bpp, b_setup], writes=[b_pB[x]])
                            P.op("act", lambda e: e.activation(out=pT[:], in_=pBt[x][:, 0:256], func=AF.Copy),
                                 reads=[b_pB[x]], writes=[bpT])

                        def a3(h, tm=tm, tb=tb, c=c):
                            kv = h // 8
                            r = h % 5
                            pT, bpT = tm["pT"][r], tb["pT"][r]
                            for half in range(2):
                                P.op("pe", lambda e, half=half: e.matmul(
                                    out=pgB[:, h * 64:(h + 1) * 64], lhsT=pT[:, half * 128:(half + 1) * 128],
                                    rhs=vtok[li][:, c + half, kv * 64:(kv + 1) * 64], start=(half == 0), stop=(half == 1)),
                                    reads=[bpT, b_vtok[li]], writes=[b_pg[2 + h // 8]])

                        def rms_part1(tm=tm, tb=tb, c=c):
                            ssq = tm["ssq"]
                            P.op("dve", lambda e, ssq=ssq: e.tensor_scalar(out=ssq[:, 4:8], in0=ssq[:, 0:4], scalar1=1.0 / 512,
                                                                           scalar2=RMS_EPS, op0=ALU.mult, op1=ALU.add),
                                 reads=[tb["ssq"]], writes=[tb["ssq"]])
                            P.op("act", lambda e, ssq=ssq: e.activation(out=ssq[:, 4:8], in_=ssq[:, 4:8], func=AF.Sqrt),
                                 reads=[tb["ssq"]], writes=[tb["ssq"]])
                            P.op("dve", lambda e, ssq=ssq: e.reciprocal(out=ssq[:, 8:12], in_=ssq[:, 4:8]),
                                 reads=[tb["ssq"]], writes=[tb["ssq"]])
                            for gg in range(NGR):
                                P.op("act", lambda e, tm=tm, gg=gg, ssq=ssq: e.activation(
                                    out=tm["yn"][:, gg * 512:(gg + 1) * 512], in_=tm["y"][:, gg * 512:(gg + 1) * 512], func=AF.Copy,
                                    scale=ssq[:, 8 + gg:9 + gg]), reads=[tb["yg"][gg], tb["ssq"]], writes=[tb["yn"]])
                            if g == 0 and li == 0 and c == 0:
                                dump("y", tm["y"][:], tb["yg"], [128, DI])
                                dump("X", tm["X"][:], [tb["X"]], [128, DI], BF16)
                                dump("sm", tm["sm"][:], [tb["sm"]], [128, 5, NHS])
                                dump("CBm", tm["CBm"][:], [tb["CBm"]], [128, NGR, 128], BF16)
                                ckpt(3)

                        for t in range(AH + 4):
                            if t == 4:
                                rms_part1()
                            if t < AH:
                                a1(t)
                            if 0 <= t - 2 < AH:
                                a2(t - 2)
                            if 0 <= t - 4 < AH:
                                a3(t - 4)

                        yield
                        for cb in range(16):
                            P.op("pe", lambda e, cb=cb, tm=tm: e.transpose(
                                out=pBt[cb // 8][:, (cb % 8) * 128:(cb % 8 + 1) * 128], in_=tm["yn"][:, cb * 128:(cb + 1) * 128],
                                identity=identb[:]), reads=[tb["yn"], b_setup], writes=[b_pB[cb // 8]])
                        for hf in range(2):
                            P.op("dve", lambda e, hf=hf, cs=cs: e.tensor_tensor(
                                out=yT[:, hf * 8:(hf + 1) * 8, cs], in0=pBt[hf][:, :].rearrange("p (j t) -> p j t", t=128),
                                in1=pcol(l, PL_NORMW, 8, hf * 8).unsqueeze(2).to_broadcast([128, 8, 128]), op=ALU.mult),
                                reads=[b_pB[hf], b_pp], writes=[b_yT])

                        P.op("dve", lambda e, ast=ast: e.tensor_tensor(out=ast[:, 3, :], in0=ast[:, 1, :],
                                                                       in1=pcol(l, PL_SINK, AH), op=ALU.add),
                             reads=tb["asth"] + [b_pp], writes=[tb["ast"]])
                        P.op("act", lambda e, ast=ast: e.activation(out=ast[:, 3, :], in_=ast[:, 3, :], func=AF.Exp),
                             reads=[tb["ast"]], writes=[tb["ast"]])
                        P.op("dve", lambda e, ast=ast: e.tensor_tensor(out=ast[:, 4, :], in0=ast[:, 3, :], in1=ast[:, 2, :],
                                                                       op=ALU.add), reads=[tb["ast"]], writes=[tb["ast"]])
                        P.op("dve", lambda e, ast=ast: e.reciprocal(out=ast[:, 5, :], in_=ast[:, 4, :]),
                             reads=[tb["ast"]], writes=[tb["ast"]])
                        for hf in range(2):
                            P.op("dve", lambda e, tm=tm, ast=ast, hf=hf: e.tensor_tensor(
                                out=tm["att"][:, hf * 512:(hf + 1) * 512].rearrange("p (h d) -> p h d", d=64),
                                in0=pgB[:, hf * 512:(hf + 1) * 512].rearrange("p (h d) -> p h d", d=64),
                                in1=ast[:, 5, hf * 8:(hf + 1) * 8].unsqueeze(2).to_broadcast([128, 8, 64]), op=ALU.mult),
                                reads=[b_pg[2 + hf], tb["ast"]], writes=[tb["att"]])
                        if g == 0 and li == 0 and c == 0:
                            dump("att", tm["att"][:], [tb["att"]], [128, D], BF16)
                            ckpt(4)
                        for j in range(8):
                            P.op("pe", lambda e, tm=tm, j=j: e.transpose(out=pBt[0][:, j * 128:(j + 1) * 128],
                                                                         in_=tm["att"][:, j * 128:(j + 1) * 128],
                                                                         identity=identb[:]),
                                 reads=[tb["att"], b_setup], writes=[b_pB[0]])
                        P.op("act", lambda e, cs=cs: e.activation(out=attT[:, :, cs],
                                                                  in_=pBt[0][:, :].rearrange("p (j t) -> p j t", t=128),
                                                                  func=AF.Copy), reads=[b_pB[0]], writes=[b_attT])
                gens = [chunk_gen(c) for c in range(NCH)]
                next(gens[0])
                for c in range(NCH):
                    next(gens[c])
                    if c + 1 < NCH:
                        next(gens[c + 1])
                    for _ in gens[c]:
                        pass

                P.op("dve", lambda e: e.tensor_copy(out=kT[li][:, :, 0:T], in_=kT[li][:, :, NT:NT + T]),
                     reads=[b_kT[li]], writes=[b_kT[li]])
                P.op("dve", lambda e: e.tensor_copy(out=vtok[li][:, 0, :], in_=vtok[li][:, NCH, :]),
                     reads=[b_vtok[li]], writes=[b_vtok[li]])
                if g == 0 and li == 0:
                    dump("yT", yT[:], [b_yT], [128, 16, NT], BF16)
                    dump("attT", attT[:], [b_attT], [128, 8, NT], BF16)
                    ckpt(5)

                for j in range(8):
                    slotA = cur.next()
                    k0, k1, k2, k3 = nextbank(), nextbank(), nextbank(), nextbank()
                    gemm_b(slotA, 0, 8, hT, b_hTj, k0)
                    gemm_b(slotA, 1024, 8, hT, b_hTj, k1)
                    gemm_b(slotA, 2048, 8, attT, b_attT, k2)
                    cur.done()
                    slotB = cur.next()
                    gemm_b(slotB, 0, 16, yT, b_yT, k3)
                    cur.done()
                    P.op("act", lambda e: e.activation(out=gtmp[0][:], in_=pg(k0)[:, 0:NT], func=AF.Sigmoid),
                         reads=[b_pg[k0]], writes=[b_gtmp[0]])
                    P.op("act", lambda e: e.activation(out=gtmp[1][:], in_=pg(k1)[:, 0:NT], func=AF.Sigmoid),
                         reads=[b_pg[k1]], writes=[b_gtmp[1]])
                    P.op("dve", lambda e: e.tensor_tensor(out=gtmp[2][:], in0=gtmp[0][:], in1=pg(k3)[:, 0:NT], op=ALU.mult),
                         reads=[b_gtmp[0], b_pg[k3]], writes=[b_gtmp[2]])
                    P.op("dve", lambda e: e.tensor_tensor(out=gtmp[3][:], in0=gtmp[1][:], in1=pg(k2)[:, 0:NT], op=ALU.mult),
                         reads=[b_gtmp[1], b_pg[k2]], writes=[b_gtmp[3]])
                    P.op("dve", lambda e, j=j: e.tensor_tensor(out=mixT[:, j, :], in0=gtmp[2][:], in1=gtmp[3][:],
                                                               op=ALU.add),
                         reads=[b_gtmp[2], b_gtmp[3]], writes=[b_mixT])
                if g == 0 and li == 0:
                    dump("mixT", mixT[:], [b_mixT], [128, 8, NT], BF16)
                    ckpt(6)

                for s in range(2):
                    slot = cur.next()
                    for bi in range(4):
                        j = s * 4 + bi
                        bank = nextbank()
                        gemm_b(slot, bi * 1024, 8, mixT, b_mixT, bank)
                        P.op("dve", lambda e, j=j, bank=bank: e.scalar_tensor_tensor(
                            out=rT[:, j, :], in0=hresT[:, j, :], scalar=ALPHA, in1=pg(bank)[:, 0:NT], op0=ALU.mult,
                            op1=ALU.add), reads=[b_hres, b_pg[bank]], writes=[b_rTj[j]])
                        ln_pre(j)
                    cur.done()
                ln_feature(l, PL_LMG, PL_LMB)
                if g == 0 and li == 0:
                    dump("h1T", hresT[:], [b_hres], [128, KD, NT])
                    ckpt(7)

                for s in range(11):
                    slot = cur.next()
                    for bi in range(2):
                        i = s * 2 + bi
                        kg, ku = nextbank(), nextbank()
                        gemm_b(slot, bi * 2048, 8, hT, b_hTj, kg)
                        gemm_b(slot, bi * 2048 + 1024, 8, hT, b_hTj, ku)
                        x = i % 4
                        P.op("act", lambda e, x=x: e.activation(out=gtmp[x][:], in_=pg(kg)[:, 0:NT], func=AF.Silu),
                             reads=[b_pg[kg]], writes=[b_gtmp[x]])
                        P.op("dve", lambda e, x=x, i=i: e.tensor_tensor(out=hidT[:, i, :], in0=gtmp[x][:],
                                                                        in1=pg(ku)[:, 0:NT], op=ALU.mult),
                             reads=[b_gtmp[x], b_pg[ku]], writes=[b_hid[i]])
                    cur.done()
                if li == NL - 1 and g + 1 < ngroups:
                    p0a(g + 1)
                for j in range(8):
                    slot = cur.next()
                    bank = nextbank()
                    gemm_b(slot, 0, KF, hidT, b_hid, bank)
                    P.op("dve", lambda e, j=j, bank=bank: e.scalar_tensor_tensor(
                        out=rT[:, j, :], in0=hresT[:, j, :], scalar=ALPHA, in1=pg(bank)[:, 0:NT], op0=ALU.mult,
                        op1=ALU.add), reads=[b_hres, b_pg[bank]], writes=[b_rTj[j]])
                    ln_pre(j)
                    cur.done()
                ln_feature(l, PL_LFG, PL_LFB)
                if g == 0 and li == 0:
                    dump("h2T", hresT[:], [b_hres], [128, KD, NT])

            for c in range(NCH):
                gc = g * NCH + c
                ob, bo = osb[0], b_osb[0]
                for j in range(KD):
                    P.op("pe", lambda e, j=j, c=c: e.transpose(out=pgA[:, j * 128:(j + 1) * 128],
                                                               in_=hresT[:, j, c * T:(c + 1) * T], identity=cc(C_ID)),
                         reads=[b_hres, b_cst], writes=[b_pg[j // 4]])
                P.op("act", lambda e, ob=ob: e.activation(out=ob[:, 0:512], in_=pgA[:, 0:512], func=AF.Copy),
                     reads=[b_pg[0]], writes=[bo])
                P.op("dve", lambda e, ob=ob: e.tensor_copy(out=ob[:, 512:1024], in_=pgA[:, 512:1024]),
                     reads=[b_pg[1]], writes=[bo])
                byo = Buf(f"yo{gc}")
                P.op("pool", lambda e, ob=ob, gc=gc: e.dma_start(out=yout[gc * T:(gc + 1) * T, :], in_=ob),
                     reads=[bo], writes=[byo], dma=True)
                out_bufs.append(byo)

        P.op("sp", lambda e: e.nop(), reads=out_bufs)
        P.emit(st)
    return nc, dbg_out


_CACHE = {}


def _run(layers, first, x_list, inp, nch=2, debug=None):
    ntok = x_list[0].shape[0]
    key = (tuple(layers), first, ntok, nch, None if debug is None else tuple(debug))
    if key not in _CACHE:
        _CACHE[key] = build_program(layers, first, ntok // T, nch=nch, debug=debug)
    nc, dbg = _CACHE[key]
    wkeys = ("w_in", "w_ssd_out", "w_att_out", "w_mix_out", "w_ffn_gate", "w_ffn_up", "w_ffn_down")
    ws = [build_wstream(l, *[inp[k] for k in wkeys]) for l in layers]
    pp = build_params(inp)
    cst = build_consts()
    in_maps = []
    for xl in x_list:
        m = {"xin": np.ascontiguousarray(xl, dtype=np.float32), "pp": pp, "cst": cst}
        for i in range(len(layers)):
            m[f"wst{i}"] = ws[i]
        in_maps.append(m)
    res = run_bass_kernel_spmd(nc, in_maps, core_ids=list(range(len(x_list))))
    return res.results


def kernel(**inputs):
    inp = {k: np.asarray(v) for k, v in inputs.items()}
    x = inp["x"].astype(np.float32)
    xs = [x[b] for b in range(BATCH)]
    res = _run([0, 1], True, xs, inp)
    out = np.stack([r["yout"] for r in res], axis=0)
    return out.astype(np.float32)
```

```python
import contextlib
import types
import numpy as np
import concourse.bass as bass
import concourse.mybir as mybir
from concourse.bass_utils import run_bass_kernel_spmd

F32 = mybir.dt.float32
BF16 = mybir.dt.bfloat16
AF = mybir.ActivationFunctionType
ALU = mybir.AluOpType
AX = mybir.AxisListType

D = 1024
KD = 8
T = 128
DEPTH = 2
SEQ = 8192
BATCH = 4
DI = 2048
NHS = 32
HP = 64
NGR = 4
FF = 2816
KF = 22
AH = 16
O_Q, O_K, O_V, O_Z, O_XS, O_B, O_C, O_DT, O_GA, O_GB = 0, 1024, 1152, 1280, 3328, 5376, 5888, 6400, 6432, 7456
ALPHA = float((2 * DEPTH) ** 0.25)
LN_EPS = 1e-5
RMS_EPS = 1e-5
NEG = -30000.0
SLAB = 4096
NW = 3

ENGS = ("pe", "act", "dve", "pool", "sp")


class Buf:
    __slots__ = ("name", "last_writer", "readers", "excl")

    def __init__(self, name, excl=False):
        self.name = name
        self.last_writer = None
        self.readers = []
        self.excl = excl


class Op:
    __slots__ = ("eng", "fn", "deps", "is_dma", "idx", "signal", "sig_val", "sem", "dma_val", "dma_prev")

    def __init__(self, eng, fn, is_dma):
        self.eng = eng
        self.fn = fn
        self.deps = []
        self.is_dma = is_dma
        self.idx = 0
        self.signal = False
        self.sig_val = 0
        self.sem = None
        self.dma_val = 0
        self.dma_prev = None


def _freeze(fn):
    if fn.__closure__ is None:
        return fn
    cells = []
    for c in fn.__closure__:
        try:
            cells.append(types.CellType(c.cell_contents))
        except ValueError:
            cells.append(c)
    return types.FunctionType(fn.__code__, fn.__globals__, fn.__name__, fn.__defaults__, tuple(cells))


class Prog:
    NDMASEM = 6
    SEM_WRAP = 30000

    def __init__(self, nc):
        self.nc = nc
        self.ops = {e: [] for e in ENGS}
        self.dma_hist = {e: [] for e in ENGS}

    halt = False

    def op(self, eng, fn, reads=(), writes=(), dma=False):
        if self.halt:
            return None
        o = Op(eng, _freeze(fn), dma)
        deps = []
        for b in reads:
            if b.last_writer is not None:
                deps.append(b.last_writer)
            if b.excl:
                deps.extend(r for r in b.readers if r.eng != eng)
        for b in writes:
            if b.last_writer is not None:
                deps.append(b.last_writer)
            deps.extend(b.readers)
        for b in reads:
            b.readers.append(o)
        for b in writes:
            b.last_writer = o
            b.readers = []
        seen = set()
        for d in deps:
            if d is o or id(d) in seen:
                continue
            seen.add(id(d))
            if (not d.is_dma) and d.eng == "pe" and eng == "pe" and not dma:
                continue
            o.deps.append(d)
            if not d.is_dma:
                d.signal = True
        if dma:
            hist = self.dma_hist[eng]
            o.idx = len(hist)
            if o.idx >= self.NDMASEM:
                o.dma_prev = hist[o.idx - self.NDMASEM]
            hist.append(o)
        self.ops[eng].append(o)
        return o

    def emit(self, st):
        nc = self.nc
        esem, dsem = {}, {}
        for e in ENGS:
            n = sum(1 for o in self.ops[e] if o.signal and not o.is_dma)
            esem[e] = [st.enter_context(nc.semaphore(f"s_{e}_{k}")) for k in range(max(1, -(-n // self.SEM_WRAP)))]
            if self.dma_hist[e]:
                dsem[e] = [st.enter_context(nc.semaphore(f"d_{e}_{k}")) for k in range(self.NDMASEM)]
        for e in ENGS:
            c = 0
            for o in self.ops[e]:
                if o.is_dma:
                    o.sem = dsem[e][o.idx % self.NDMASEM]
                    o.dma_val = 16 * (o.idx // self.NDMASEM + 1)
                elif o.signal:
                    o.sem = esem[e][c // self.SEM_WRAP]
                    o.sig_val = c % self.SEM_WRAP + 1
                    c += 1
        block = st.enter_context(nc.Block())

        def run(engname, engobj):
            waited = {}
            for o in self.ops[engname]:
                ws = {}
                deps = o.deps if o.dma_prev is None else o.deps + [o.dma_prev]
                for d in deps:
                    val = d.dma_val if d.is_dma else d.sig_val
                    key = id(d.sem)
                    if waited.get(key, 0) >= val:
                        continue
                    if key not in ws or ws[key][1] < val:
                        ws[key] = (d.sem, val)
                for key, (sem, val) in ws.items():
                    engobj.wait_ge(sem, val)
                    waited[key] = val
                ins = o.fn(engobj)
                if o.is_dma:
                    ins.then_inc(o.sem, 16)
                elif o.signal:
                    ins.then_inc(o.sem, 1)

        @block.tensor
        def _(e):
            run("pe", e)

        @block.scalar
        def _(e):
            run("act", e)

        @block.vector
        def _(e):
            run("dve", e)

        @block.gpsimd
        def _(e):
            run("pool", e)

        @block.sync
        def _(e):
            run("sp", e)


def _blk(W, col0, ncols, kc):
    return np.ascontiguousarray(
        W[:kc * 128, col0:col0 + ncols].reshape(kc, 128, ncols).transpose(1, 0, 2).reshape(128, kc * ncols))


def slab_plan():
    slabs = []
    b1 = [("q", j) for j in range(8)] + [("kd", kv) for kv in range(2)] + [("xbc", cb) for cb in range(24)]
    for i in range(0, len(b1), 4):
        slabs.append([(t, a, 128, 8) for (t, a) in b1[i:i + 4]])
    for s in range(4):
        slabs.append([("z", s, 512, 8)])
    slabs.append([("vdt", 0, 160, 8)])
    for j in range(8):
        slabs.append([("ga", j, 128, 8), ("gb", j, 128, 8), ("ao", j, 128, 8)])
        slabs.append([("so", j, 128, 16)])
    for i in range(0, 8, 4):
        slabs.append([("mo", j, 128, 8) for j in range(i, i + 4)])
    for i in range(0, 22, 2):
        slabs.append([("fg", i, 128, 8), ("fu", i, 128, 8), ("fg", i + 1, 128, 8), ("fu", i + 1, 128, 8)])
    for j in range(8):
        slabs.append([("fd", j, 128, 22)])
    return slabs


def slab_sizes():
    return [sum(nc_ * kc for (_, _, nc_, kc) in s) for s in slab_plan()]


def build_wstream(l, w_in, w_ssd_out, w_att_out, w_mix_out, w_ffn_gate, w_ffn_up, w_ffn_down):
    wi = w_in[l]
    kd = [np.concatenate([wi[:, O_K + kv * 64:O_K + kv * 64 + 64]] * 2, axis=1) for kv in range(2)]
    vdt = np.concatenate([wi[:, O_V:O_V + 128], wi[:, O_DT:O_DT + 32]], axis=1)
    parts = []
    for s in slab_plan():
        for (t, a, ncols, kc) in s:
            if t == "q":
                parts.append(_blk(wi, O_Q + a * 128, 128, 8))
            elif t == "kd":
                parts.append(_blk(kd[a], 0, 128, 8))
            elif t == "xbc":
                parts.append(_blk(wi, O_XS + a * 128, 128, 8))
            elif t == "z":
                parts.append(_blk(wi, O_Z + a * 512, 512, 8))
            elif t == "vdt":
                parts.append(_blk(vdt, 0, 160, 8))
            elif t == "ga":
                parts.append(_blk(wi, O_GA + a * 128, 128, 8))
            elif t == "gb":
                parts.append(_blk(wi, O_GB + a * 128, 128, 8))
            elif t == "ao":
                parts.append(_blk(w_att_out[l], a * 128, 128, 8))
            elif t == "so":
                parts.append(_blk(w_ssd_out[l], a * 128, 128, 16))
            elif t == "mo":
                parts.append(_blk(w_mix_out[l], a * 128, 128, 8))
            elif t == "fg":
                parts.append(_blk(w_ffn_gate[l], a * 128, 128, 8))
            elif t == "fu":
                parts.append(_blk(w_ffn_up[l], a * 128, 128, 8))
            elif t == "fd":
                parts.append(_blk(w_ffn_down[l], a * 128, 128, 22))
    return np.ascontiguousarray(np.concatenate(parts, axis=1), dtype=np.float32)


PL_CONVW, PL_CONVB, PL_DTB, PL_ALOG, PL_DSK, PL_SINK, PL_NORMW, PL_LMG, PL_LMB, PL_LFG, PL_LFB, PL_N = \
    0, 96, 120, 152, 184, 216, 232, 248, 256, 264, 272, 280
PG_N = 16


def _fm(v):
    return np.ascontiguousarray(v.reshape(-1, 128).T)


def _bc(v):
    return np.ascontiguousarray(np.broadcast_to(v[None, :], (128, v.shape[0])))


def build_params(inp):
    cols = [_fm(inp["ln_in_g"]), _fm(inp["ln_in_b"])]
    for l in range(DEPTH):
        cw = inp["conv_w"][l]
        cwl = cw.T.reshape(24, 128, 4).transpose(1, 0, 2).reshape(128, 96)
        cols += [cwl, _fm(inp["conv_b"][l]), _bc(inp["dt_bias"][l]), _bc(inp["a_log"][l]), _bc(inp["d_skip"][l]),
                 _bc(inp["att_sinks"][l]), _fm(inp["ssd_norm_w"][l]), _fm(inp["ln_mix_g"][l]), _fm(inp["ln_mix_b"][l]),
                 _fm(inp["ln_ffn_g"][l]), _fm(inp["ln_ffn_b"][l])]
    return np.ascontiguousarray(np.concatenate(cols, axis=1), dtype=np.float32)


C_ID, C_U, C_L, C_ONESD, C_ONES, C_R0, C_M, C_M0, C_N = 0, 128, 256, 384, 512, 640, 896, 1152, 1408


def build_consts():
    i = np.arange(128)
    ident = (i[:, None] == i[None, :]).astype(np.float32)
    U = (i[:, None] <= i[None, :]).astype(np.float32)
    Lm = (i[:, None] > i[None, :]).astype(np.float32)
    onesd = np.full((128, 128), 1.0 / D, np.float32)
    ones = np.ones((128, 128), np.float32)
    s = np.arange(256)
    rel = i[:, None] + 128 - s[None, :]
    R0 = (-rel).astype(np.float32)
    valid = (rel >= 0) & (rel < 128)
    M = np.where(valid, 0.0, NEG).astype(np.float32)
    M0 = np.where(valid & (s[None, :] >= 128), 0.0, NEG).astype(np.float32)
    return np.ascontiguousarray(np.concatenate([ident, U, Lm, onesd, ones, R0, M, M0], axis=1))


SLOPES = [float(2.0 ** (-8.0 * (h + 1) / AH)) for h in range(AH)]


class _Stop(Exception):
    pass


def build_program(layers, first, nchunks, nch=2, debug=None, stop=None):
    NCH = nch
    NT = NCH * T
    ngroups = nchunks // NCH
    NL = len(layers)
    nc = bass.Bass("TRN2", target_bir_lowering=False)
    ssz = slab_sizes()
    plan = slab_plan()
    WTOT = sum(ssz)
    xin = nc.dram_tensor("xin", [nchunks * T, D], F32, kind="ExternalInput").ap()
    wst = [nc.dram_tensor(f"wst{i}", [128, WTOT], F32, kind="ExternalInput").ap() for i in range(NL)]
    ppd = nc.dram_tensor("pp", [128, PG_N + DEPTH * PL_N], F32, kind="ExternalInput").ap()
    cstd = nc.dram_tensor("cst", [128, C_N], F32, kind="ExternalInput").ap()
    yout = nc.dram_tensor("yout", [nchunks * T, D], F32, kind="ExternalOutput").ap()
    dbg_out = {}

    with contextlib.ExitStack() as st:
        P = Prog(nc)

        def sb(name, shape, dt=F32):
            return st.enter_context(nc.sbuf_tensor(name, shape, dt))

        pp = sb("pp_sb", [128, PG_N + DEPTH * PL_N]); b_pp = Buf("pp")
        cst = sb("cst_sb", [128, C_N]); b_cst = Buf("cst")
        identb = sb("identb", [128, 128], BF16)
        maskb = sb("maskb", [128, 256], BF16)
        mask0b = sb("mask0b", [128, 256], BF16)
        aneg = sb("aneg", [128, NL, NHS])
        hresT = sb("hresT", [128, KD, NT]); b_hres = Buf("hres")
        rT = sb("rT", [128, KD, NT]); b_rTj = [Buf(f"rT{j}") for j in range(KD)]
        onesdb = sb("onesdb", [128, 128], BF16)
        sqT = sb("sqT", [128, KD, NT]); b_sq = Buf("sq")
        hT = sb("hT", [128, KD, NT], BF16); b_hTj = [Buf(f"hT{j}") for j in range(KD)]
        Hst = [sb(f"Hst{i}", [128, NGR, 512]) for i in range(NL)]; b_Hst = [[Buf(f"Hst{i}_{g}") for g in range(NGR)] for i in range(NL)]
        Hbf = [sb(f"Hbf{i}", [128, NGR, 512], BF16) for i in range(NL)]; b_Hbf = [[Buf(f"Hbf{i}_{g}") for g in range(NGR)] for i in range(NL)]
        ctail = [sb(f"ctail{i}", [128, 24, 3]) for i in range(NL)]; b_ctail = [[Buf(f"ct{i}_{a}") for a in range(24)] for i in range(NL)]
        kT = [sb(f"kT{i}", [128, 2, T + NT], BF16) for i in range(NL)]; b_kT = [Buf(f"kT{i}") for i in range(NL)]
        vtok = [sb(f"vtok{i}", [128, NCH + 1, 128], BF16) for i in range(NL)]; b_vtok = [Buf(f"vt{i}") for i in range(NL)]
        wring = [sb(f"wring{i}", [128, SLAB], BF16) for i in range(NW)]; b_w = [Buf(f"w{i}") for i in range(NW)]
        qT = sb("qT", [128, 8, NT], BF16); b_qT = Buf("qT")
        xbcT = sb("xbcT", [128, 24, NT], BF16); b_xbcj = [Buf(f"xbc{j}") for j in range(24)]
        ust = [sb(f"ust{i}", [128, NT + 3]) for i in range(3)]; b_ust = [Buf(f"ust{i}") for i in range(3)]
        cacc = [sb(f"cacc{i}", [128, NT]) for i in range(3)]; b_cacc = [Buf(f"cacc{i}") for i in range(3)]
        zs = sb("zs", [128, NCH, DI], BF16); b_zs = Buf("zs")
        dtt = sb("dtt", [128, NCH, NHS]); dAt = sb("dAt", [128, NCH, NHS]); b_dt = Buf("dt")
        spt = sb("spt", [128, 4, NHS]); b_spt = Buf("spt")
        yT = sb("yT", [128, 16, NT], BF16); b_yT = Buf("yT")
        attT = sb("attT", [128, 8, NT], BF16); b_attT = Buf("attT")
        mixT = qT; b_mixT = b_qT
        hidT = xbcT; b_hid = b_xbcj
        gtmp = [sb(f"gtmp{i}", [128, NT]) for i in range(4)]; b_gtmp = [Buf(f"gtmp{i}") for i in range(4)]
        lnt = sb("lnt", [128, 2, NT]); b_lnt = Buf("lnt")
        xld = [sb(f"xld{i}", [128, D]) for i in range(NCH)]; b_xld = [Buf(f"xld{i}") for i in range(NCH)]
        xst = [sb(f"xst{i}", [128, 8]) for i in range(NCH)]; b_xst = [Buf(f"xst{i}") for i in range(NCH)]
        osb = [sqT[:, :, :].rearrange("p k t -> p (k t)")[:, D:2 * D]]; b_osb = [Buf("osb")]
        TM = []
        NTM = 1
        for i in range(NTM):
            d = dict(
                sm=sb(f"sm{i}", [128, 5, NHS]),
                X=sb(f"X{i}", [128, DI], BF16), Xd=sb(f"Xd{i}", [128, DI], BF16), xs=sb(f"xs{i}", [128, DI], BF16),
                Dx=sb(f"Dx{i}", [128, DI], BF16),
                Btok=sb(f"Btok{i}", [128, 512], BF16), CBm=sb(f"CBm{i}", [128, NGR, 128], BF16),
                Rb=[sb(f"Rb{i}_{j}", [128, 4, 128]) for j in range(3)],
                LT=[sb(f"LT{i}_{j}", [128, 4, 128], BF16) for j in range(3)],
                MT=[sb(f"MT{i}_{j}", [128, 4, 128], BF16) for j in range(3)],
                t1=[sb(f"t1_{i}_{j}", [128, 512]) for j in range(2)], y=sb(f"y{i}", [128, DI]),
                ssq=sb(f"ssq{i}", [128, 12]), yn=sb(f"yn{i}", [128, DI], BF16),
                sc=[sb(f"sc{i}_{j}", [128, 256]) for j in range(3)],
                pp_=[sb(f"p{i}_{j}", [128, 256], BF16) for j in range(5)],
                pT=[sb(f"pT{i}_{j}", [128, 256], BF16) for j in range(5)],
                ast=sb(f"ast{i}", [128, 6, AH]),
                att=sb(f"att{i}", [128, D], BF16),
            )
            d["b"] = {k: Buf(f"{k}{i}") for k in ("sm", "X", "Xd", "xs", "Dx", "Btok", "CBm", "t1", "y", "junk", "ssq",
                                                    "yn", "ast", "att")}
            d["b"]["asth"] = [Buf(f"asth{i}_{j}") for j in range(AH)]
            d["b"]["yg"] = [Buf(f"yg{i}_{j}") for j in range(NGR)]
            d["b"]["t1"] = [Buf(f"t1_{i}_{j}") for j in range(2)]
            for k in ("Rb", "LT", "MT", "sc", "pp_", "pT"):
                d["b"][k] = [Buf(f"{k}{i}_{j}") for j in range(5)]
            TM.append(d)

        pgA = st.enter_context(nc.psum_tensor("pgA", [128, 1024], F32))
        pgB = st.enter_context(nc.psum_tensor("pgB", [128, 1024], F32))
        pD = [st.enter_context(nc.psum_tensor(f"pD{i}", [128, 512], F32)) for i in range(2)]
        pBt = [st.enter_context(nc.psum_tensor(f"pBt{i}", [128, 1024], BF16)) for i in range(2)]
        b_pD = [Buf(f"pD{i}", True) for i in range(2)]
        b_pg = [Buf(f"pg{i}", True) for i in range(4)] + b_pD
        b_pB = [Buf(f"pB{i}", True) for i in range(2)]

        def pg(i):
            if i >= 4:
                return pD[i - 4][:, :]
            t = pgA if i < 2 else pgB
            o = (i % 2) * 512
            return t[:, o:o + 512]

        def cc(a):
            return cst[:, a:a + 128]

        def pcol(l, off, n=1, j=0):
            o = PG_N + l * PL_N + off + j
            return pp[:, o:o + n]

        out_bufs = []

        P.op("sp", lambda e: e.dma_start(out=pp[:], in_=ppd), writes=[b_pp], dma=True)
        P.op("sp", lambda e: e.dma_start(out=cst[:], in_=cstd), writes=[b_cst], dma=True)
        b_setup = Buf("setup")
        P.op("dve", lambda e: e.tensor_copy(out=identb[:], in_=cst[:, C_ID:C_ID + 128]), reads=[b_cst], writes=[b_setup])
        P.op("dve", lambda e: e.tensor_copy(out=onesdb[:], in_=cst[:, C_ONESD:C_ONESD + 128]), reads=[b_cst], writes=[b_setup])
        P.op("dve", lambda e: e.tensor_copy(out=maskb[:], in_=cst[:, C_M:C_M + 256]), reads=[b_cst], writes=[b_setup])
        P.op("dve", lambda e: e.tensor_copy(out=mask0b[:], in_=cst[:, C_M0:C_M0 + 256]), reads=[b_cst], writes=[b_setup])
        for i, l in enumerate(layers):
            P.op("act", lambda e, i=i, l=l: e.activation(out=aneg[:, i, :], in_=pcol(l, PL_ALOG, NHS), func=AF.Exp),
                 reads=[b_pp], writes=[b_setup])
            P.op("dve", lambda e, i=i: e.tensor_scalar(out=aneg[:, i, :], in0=aneg[:, i, :], scalar1=-1.0, scalar2=None,
                                                       op0=ALU.mult), reads=[b_setup], writes=[b_setup])
            P.op("dve", lambda e, i=i: e.memset(Hst[i][:], 0.0), writes=b_Hst[i])
            P.op("dve", lambda e, i=i: e.memset(Hbf[i][:], 0.0), writes=b_Hbf[i])
            P.op("dve", lambda e, i=i: e.memset(ctail[i][:], 0.0), writes=b_ctail[i])
            P.op("dve", lambda e, i=i: e.memset(kT[i][:], 0.0), writes=[b_kT[i]])
            P.op("dve", lambda e, i=i: e.memset(vtok[i][:], 0.0), writes=[b_vtok[i]])

        wstate = {"n": 0}
        nslab = len(plan)
        seq_slabs = []
        for g in range(ngroups):
            for i in range(NL):
                for s in range(nslab):
                    seq_slabs.append((i, s))
        soff = np.concatenate([[0], np.cumsum(ssz)]).astype(int)

        wbf = [nc.dram_tensor(f"wbf{i}", [128, WTOT], BF16, kind="Internal").ap() for i in range(NL)]
        b_wbf = [[Buf(f"wbf{i}_{s}") for s in range(nslab)] for i in range(NL)]
        conv_order = [(i, s) for i in range(NL) for s in range(nslab)]
        cvn = {"n": 0}

        def conv_more(k):
            for _ in range(k):
                if cvn["n"] >= len(conv_order):
                    return
                i, s = conv_order[cvn["n"]]
                cvn["n"] += 1
                P.op("pool", lambda e, i=i, s=s: e.dma_start(
                    out=wbf[i][:, int(soff[s]):int(soff[s + 1])], in_=wst[i][:, int(soff[s]):int(soff[s + 1])],
                    max_dma_last_dim=8192), writes=[b_wbf[i][s]], dma=True)

        def issue_slab(n):
            if n >= len(seq_slabs):
                return
            i, s = seq_slabs[n]
            slot = n % NW
            P.op("sp", lambda e, i=i, s=s, slot=slot: e.dma_start(
                out=wring[slot][:, 0:ssz[s]], in_=wbf[i][:, int(soff[s]):int(soff[s + 1])]),
                reads=[b_wbf[i][s]], writes=[b_w[slot]], dma=True)

        def start_stream():
            conv_more(NW + 5)
            for n in range(NW):
                issue_slab(n)

        class SlabCursor:
            def __init__(self):
                self.n = -1

            def next(self):
                self.n += 1
                return self.n % NW

            def done(self):
                conv_more(1)
                issue_slab(self.n + NW)
        cur = SlabCursor()

        def ckpt(n):
            if stop is not None and stop == n:
                P.halt = True

        def dump(name, ap, bufs, shape, dt=F32):
            if debug is None or name not in debug:
                return
            t = nc.dram_tensor("dbg_" + name, shape, dt, kind="ExternalOutput").ap()
            bo = Buf("dbgo_" + name)
            P.op("sp", lambda e: e.dma_start(out=t, in_=ap), reads=bufs, writes=[bo], dma=True)
            out_bufs.append(bo)
            dbg_out[name] = True

        def ln_pre(j):
            P.op("act", lambda e, j=j: e.activation(out=hT[:, j, :], in_=rT[:, j, :], func=AF.Copy),
                 reads=[b_rTj[j]], writes=[b_hTj[j]])

        def ln_feature(l, og, ob):
            bk0, bk1 = nextbank(), nextbank()
            for j in range(KD):
                P.op("pe", lambda e, j=j: e.matmul(out=pg(bk0)[:, 0:NT], lhsT=onesdb[:], rhs=hT[:, j, :], start=(j == 0),
                                                    stop=(j == KD - 1)), reads=[b_hTj[j], b_setup], writes=[b_pg[bk0]])
            for j in range(KD):
                P.op("dve", lambda e, j=j: e.tensor_tensor(out=rT[:, j, :], in0=rT[:, j, :], in1=pg(bk0)[:, 0:NT],
                                                           op=ALU.subtract), reads=[b_rTj[j], b_pg[bk0]], writes=[b_rTj[j]])
                P.op("act", lambda e, j=j: e.activation(out=hT[:, j, :], in_=rT[:, j, :], func=AF.Square),
                     reads=[b_rTj[j]], writes=[b_hTj[j]])
                P.op("pe", lambda e, j=j: e.matmul(out=pg(bk1)[:, 0:NT], lhsT=onesdb[:], rhs=hT[:, j, :], start=(j == 0),
                                                    stop=(j == KD - 1)), reads=[b_hTj[j], b_setup], writes=[b_pg[bk1]])
            P.op("dve", lambda e: e.tensor_scalar(out=lnt[:, 0, :], in0=pg(bk1)[:, 0:NT], scalar1=LN_EPS, scalar2=None,
                                                  op0=ALU.add), reads=[b_pg[bk1]], writes=[b_lnt])
            P.op("act", lambda e: e.activation(out=lnt[:, 0, :], in_=lnt[:, 0, :], func=AF.Sqrt),
                 reads=[b_lnt], writes=[b_lnt])
            P.op("dve", lambda e: e.reciprocal(out=lnt[:, 1, :], in_=lnt[:, 0, :]), reads=[b_lnt], writes=[b_lnt])
            for j in range(KD):
                P.op("dve", lambda e, j=j: e.tensor_tensor(out=rT[:, j, :], in0=rT[:, j, :], in1=lnt[:, 1, :], op=ALU.mult),
                     reads=[b_rTj[j], b_lnt], writes=[b_rTj[j]])
                P.op("act", lambda e, j=j: e.activation(out=hresT[:, j, :], in_=rT[:, j, :], func=AF.Identity,
                                                         scale=pcol(l, og, 1, j), bias=pcol(l, ob, 1, j)),
                     reads=[b_rTj[j], b_pp], writes=[b_hres])
                P.op("dve", lambda e, j=j: e.tensor_scalar(out=hT[:, j, :], in0=rT[:, j, :], scalar1=pcol(l, og, 1, j),
                                                            scalar2=pcol(l, ob, 1, j), op0=ALU.mult, op1=ALU.add),
                     reads=[b_rTj[j], b_pp], writes=[b_hTj[j]])

        def gemm_b(slot, woff, kc, actT, b_act, bank):
            for k in range(kc):
                ba = b_act[k] if isinstance(b_act, list) else b_act
                P.op("pe", lambda e, k=k: e.matmul(out=pg(bank)[:, 0:NT], lhsT=wring[slot][:, woff + k * 128:woff + (k + 1) * 128],
                                                    rhs=actT[:, k, :], start=(k == 0), stop=(k == kc - 1)),
                     reads=[b_w[slot], ba], writes=[b_pg[bank]])

        gbank = {"n": 0}

        def nextbank():
            gbank["n"] = (gbank["n"] + 1) % 6
            return gbank["n"]

        def p0a(g):
            for c in range(NCH):
                gc = g * NCH + c
                xb = c
                xt, bx, sx, bsx = xld[xb], b_xld[xb], xst[xb], b_xst[xb]
                P.op("pool", lambda e, xt=xt, gc=gc: e.dma_start(out=xt[:], in_=xin[gc * T:(gc + 1) * T, :]),
                     writes=[bx], dma=True)
                if first:
                    P.op("act", lambda e, xt=xt, sx=sx: e.activation(out=sqT[:, :, :].rearrange("p k t -> p (k t)")[:, 0:D], in_=xt[:], func=AF.Identity,
                                                                      accum_out=sx[:, 0:1]),
                         reads=[bx], writes=[b_sq, bsx])
                    P.op("dve", lambda e, sx=sx: e.tensor_scalar(out=sx[:, 1:2], in0=sx[:, 0:1], scalar1=-1.0 / D,
                                                                 scalar2=None, op0=ALU.mult), reads=[bsx], writes=[bsx])
                    P.op("dve", lambda e, xt=xt, sx=sx: e.tensor_scalar(out=xt[:], in0=xt[:], scalar1=sx[:, 1:2],
                                                                        scalar2=None, op0=ALU.add),
                         reads=[bx, bsx], writes=[bx])
                    P.op("act", lambda e, xt=xt, sx=sx: e.activation(out=sqT[:, :, :].rearrange("p k t -> p (k t)")[:, 0:D], in_=xt[:], func=AF.Square,
                                                                      accum_out=sx[:, 2:3]),
                         reads=[bx], writes=[b_sq, bsx])
                    P.op("dve", lambda e, sx=sx: e.tensor_scalar(out=sx[:, 3:4], in0=sx[:, 2:3], scalar1=1.0 / D,
                                                                 scalar2=LN_EPS, op0=ALU.mult, op1=ALU.add),
                         reads=[bsx], writes=[bsx])
                    P.op("act", lambda e, sx=sx: e.activation(out=sx[:, 4:5], in_=sx[:, 3:4], func=AF.Sqrt),
                         reads=[bsx], writes=[bsx])
                    P.op("dve", lambda e, sx=sx: e.reciprocal(out=sx[:, 5:6], in_=sx[:, 4:5]), reads=[bsx], writes=[bsx])
                    P.op("dve", lambda e, xt=xt, sx=sx: e.tensor_scalar(out=xt[:], in0=xt[:], scalar1=sx[:, 5:6],
                                                                        scalar2=None, op0=ALU.mult),
                         reads=[bx, bsx], writes=[bx])

        def p0b(g):
            for c in range(NCH):
                gc = g * NCH + c
                xt, bx = xld[c], b_xld[c]
                for j in range(KD):
                    P.op("pe", lambda e, xt=xt, j=j: e.transpose(out=pgA[:, j * 128:(j + 1) * 128],
                                                                 in_=xt[:, j * 128:(j + 1) * 128], identity=cc(C_ID)),
                         reads=[bx, b_cst], writes=[b_pg[j // 4]])
                for j in range(KD):
                    src = pgA[:, j * 128:(j + 1) * 128]
                    if first:
                        P.op("act", lambda e, j=j, c=c, src=src: e.activation(
                            out=hresT[:, j, c * T:(c + 1) * T], in_=src, func=AF.Identity,
                            scale=pp[:, j:j + 1], bias=pp[:, 8 + j:9 + j]),
                            reads=[b_pg[j // 4], b_pp], writes=[b_hres])
                        P.op("dve", lambda e, j=j, c=c, src=src: e.tensor_scalar(
                            out=hT[:, j, c * T:(c + 1) * T], in0=src, scalar1=pp[:, j:j + 1], scalar2=pp[:, 8 + j:9 + j],
                            op0=ALU.mult, op1=ALU.add), reads=[b_pg[j // 4], b_pp], writes=[b_hTj[j]])
                    else:
                        P.op("act", lambda e, j=j, c=c, src=src: e.activation(
                            out=hresT[:, j, c * T:(c + 1) * T], in_=src, func=AF.Identity),
                            reads=[b_pg[j // 4]], writes=[b_hres])
                        P.op("dve", lambda e, j=j, c=c, src=src: e.tensor_copy(out=hT[:, j, c * T:(c + 1) * T], in_=src),
                             reads=[b_pg[j // 4]], writes=[b_hTj[j]])

        for g in range(ngroups):
            if g == 0:
                p0a(0)
                start_stream()
            p0b(g)
            if g == 0:
                dump("h0T", hresT[:], [b_hres], [128, KD, NT])
                ckpt(0)

            for li, l in enumerate(layers):
                nblk = 0
                pendB, pendC = [], []
                for s in range(9):
                    slot = cur.next()
                    for bi, (t, a, ncols, kc) in enumerate(plan[s]):
                        bank = nextbank()
                        gemm_b(slot, bi * 1024, 8, hT, b_hTj, bank)
                        src = pg(bank)[:, 0:NT]
                        if t == "q":
                            P.op("act", lambda e, a=a, src=src: e.activation(out=qT[:, a, :], in_=src, func=AF.Copy,
                                                                              scale=0.125),
                                 reads=[b_pg[bank]], writes=[b_qT])
                        elif t == "kd":
                            P.op("dve", lambda e, a=a, src=src: e.tensor_copy(out=kT[li][:, a, T:T + NT], in_=src),
                                 reads=[b_pg[bank]], writes=[b_kT[li]])
                        else:
                            ub = nblk % 3
                            nblk += 1
                            u, bu, ac, bac = ust[ub], b_ust[ub], cacc[ub], b_cacc[ub]
                            P.op("act", lambda e, u=u, src=src: e.activation(out=u[:, 3:3 + NT], in_=src, func=AF.Copy),
                                 reads=[b_pg[bank]], writes=[bu])
                            P.op("dve", lambda e, u=u, a=a: e.tensor_copy(out=u[:, 0:3], in_=ctail[li][:, a, :]),
                                 reads=[b_ctail[li][a]], writes=[bu])

                            def stB(u=u, bu=bu, ac=ac, bac=bac, a=a):
                                P.op("pool", lambda e: e.tensor_scalar(
                                    out=ac[:], in0=u[:, 0:NT], scalar1=pcol(l, PL_CONVW, 1, a * 4 + 0),
                                    scalar2=pcol(l, PL_CONVB, 1, a), op0=ALU.mult, op1=ALU.add),
                                    reads=[bu, b_pp], writes=[bac])
                                for k in (1, 2, 3):
                                    P.op("dve", lambda e, k=k: e.scalar_tensor_tensor(
                                        out=ac[:], in0=u[:, k:k + NT], scalar=pcol(l, PL_CONVW, 1, a * 4 + k), in1=ac[:],
                                        op0=ALU.mult, op1=ALU.add), reads=[bu, bac, b_pp], writes=[bac])
                                P.op("dve", lambda e: e.tensor_copy(out=ctail[li][:, a, :], in_=u[:, NT:NT + 3]),
                                     reads=[bu], writes=[b_ctail[li][a]])

                            def stC(ac=ac, bac=bac, a=a):
                                P.op("act", lambda e: e.activation(out=xbcT[:, a, :], in_=ac[:], func=AF.Silu),
                                     reads=[bac], writes=[b_xbcj[a]])
                            if len(pendB) >= 1:
                                fb = pendB.pop(0)
                                fb[0]()
                                pendC.append(fb[1])
                            if len(pendC) >= 2:
                                pendC.pop(0)()
                            pendB.append((stB, stC))
                    cur.done()

                for s in range(4):
                    slot = cur.next()
                    for c in range(NCH):
                        bank = nextbank()
                        for k in range(KD):
                            P.op("pe", lambda e, k=k, c=c, slot=slot, bank=bank: e.matmul(
                                out=pg(bank), lhsT=hT[:, k, c * T:(c + 1) * T], rhs=wring[slot][:, k * 512:(k + 1) * 512],
                                start=(k == 0), stop=(k == KD - 1)), reads=[b_w[slot], b_hTj[k]], writes=[b_pg[bank]])
                        P.op("act", lambda e, c=c, s=s, bank=bank: e.activation(
                            out=zs[:, c, s * 512:(s + 1) * 512], in_=pg(bank), func=AF.Silu),
                            reads=[b_pg[bank]], writes=[b_zs])
                    cur.done()
                slot = cur.next()
                for c in range(NCH):
                    bank = nextbank()
                    for k in range(KD):
                        P.op("pe", lambda e, k=k, c=c, slot=slot, bank=bank: e.matmul(
                            out=pg(bank)[:, 0:160], lhsT=hT[:, k, c * T:(c + 1) * T],
                            rhs=wring[slot][:, k * 160:(k + 1) * 160], start=(k == 0), stop=(k == KD - 1)),
                            reads=[b_w[slot], b_hTj[k]], writes=[b_pg[bank]])
                    P.op("act", lambda e, c=c, bank=bank: e.activation(out=vtok[li][:, c + 1, :], in_=pg(bank)[:, 0:128],
                                                                        func=AF.Copy),
                         reads=[b_pg[bank]], writes=[b_vtok[li]])
                    P.op("dve", lambda e, bank=bank: e.tensor_tensor(out=spt[:, 0, :], in0=pg(bank)[:, 128:160],
                                                                     in1=pcol(l, PL_DTB, NHS), op=ALU.add),
                         reads=[b_pg[bank], b_pp], writes=[b_spt])
                    P.op("act", lambda e: e.activation(out=spt[:, 1, :], in_=spt[:, 0, :], func=AF.Abs),
                         reads=[b_spt], writes=[b_spt])
                    P.op("act", lambda e: e.activation(out=spt[:, 2, :], in_=spt[:, 1, :], func=AF.Exp, scale=-1.0),
                         reads=[b_spt], writes=[b_spt])
                    P.op("act", lambda e: e.activation(out=spt[:, 3, :], in_=spt[:, 2, :], func=AF.Ln, bias=1.0),
                         reads=[b_spt], writes=[b_spt])
                    P.op("dve", lambda e, c=c: e.scalar_tensor_tensor(out=dtt[:, c, :], in0=spt[:, 0, :], scalar=0.0,
                                                                      in1=spt[:, 3, :], op0=ALU.max, op1=ALU.add),
                         reads=[b_spt], writes=[b_dt])
                    P.op("dve", lambda e, c=c: e.tensor_tensor(out=dAt[:, c, :], in0=dtt[:, c, :], in1=aneg[:, li, :],
                                                               op=ALU.mult), reads=[b_dt, b_setup], writes=[b_dt])
                cur.done()
                for fb in pendB:
                    fb[0]()
                    pendC.append(fb[1])
                for fc in pendC:
                    fc()
                if g == 0 and li == 0:
                    dump("qT", qT[:], [b_qT], [128, 8, NT], BF16)
                    dump("kT", kT[li][:], [b_kT[li]], [128, 2, T + NT], BF16)
                    dump("xbcT", xbcT[:], b_xbcj, [128, 24, NT], BF16)
                    ckpt(1)
                if g == 0 and li == 0:
                    dump("zs", zs[:], [b_zs], [128, NCH, DI], BF16)
                    dump("dtt", dtt[:], [b_dt], [128, NCH, NHS])
                    dump("vtok", vtok[li][:], [b_vtok[li]], [128, NCH + 1, 128], BF16)
                    ckpt(2)

                def chunk_gen(c):
                        gc = g * NCH + c
                        tm = TM[gc % NTM]
                        tb = tm["b"]
                        cs = slice(c * T, (c + 1) * T)
                        sm = tm["sm"]
                        P.op("pe", lambda e, c=c: e.matmul(out=pg(0)[:, 0:32], lhsT=cc(C_U), rhs=dAt[:, c, :], start=True,
                                                           stop=True), reads=[b_dt, b_cst], writes=[b_pg[0]])
                        P.op("pe", lambda e, c=c: e.matmul(out=pg(0)[:, 32:64], lhsT=cc(C_ONES), rhs=dAt[:, c, :], start=True,
                                                           stop=True), reads=[b_dt, b_cst], writes=[b_pg[0]])
                        P.op("dve", lambda e, sm=sm: e.tensor_copy(out=sm[:, 0, :], in_=pg(0)[:, 0:32]),
                             reads=[b_pg[0]], writes=[tb["sm"]])
                        P.op("dve", lambda e, sm=sm: e.scalar_tensor_tensor(out=sm[:, 1, :], in0=sm[:, 0, :], scalar=-1.0,
                                                                            in1=pg(0)[:, 32:64], op0=ALU.mult, op1=ALU.add),
                             reads=[b_pg[0], tb["sm"]], writes=[tb["sm"]])
                        P.op("act", lambda e, sm=sm: e.activation(out=sm[:, 2, :], in_=sm[:, 0, :], func=AF.Exp),
                             reads=[tb["sm"]], writes=[tb["sm"]])
                        P.op("act", lambda e, sm=sm: e.activation(out=sm[:, 3, :], in_=sm[:, 1, :], func=AF.Exp),
                             reads=[tb["sm"]], writes=[tb["sm"]])
                        P.op("act", lambda e, sm=sm: e.activation(out=sm[:, 4, :], in_=pg(0)[:, 32:64], func=AF.Exp),
                             reads=[b_pg[0], tb["sm"]], writes=[tb["sm"]])
                        for cb in range(16):
                            P.op("pe", lambda e, cb=cb, cs=cs: e.transpose(
                                out=pBt[cb // 8][:, (cb % 8) * 128:(cb % 8 + 1) * 128], in_=xbcT[:, cb, cs], identity=identb[:]),
                                reads=[b_xbcj[cb], b_setup], writes=[b_pB[cb // 8]])
                        for hf in range(2):
                            src = pBt[hf][:, :].rearrange("p (h d) -> p h d", d=HP)
                            P.op("dve", lambda e, tm=tm, hf=hf, src=src, c=c: e.tensor_tensor(
                                out=tm["X"][:, hf * 1024:(hf + 1) * 1024].rearrange("p (h d) -> p h d", d=HP), in0=src,
                                in1=dtt[:, c, hf * 16:(hf + 1) * 16].unsqueeze(2).to_broadcast([128, 16, HP]), op=ALU.mult),
                                reads=[b_pB[hf], b_dt], writes=[tb["X"]])
                            P.op("act", lambda e, tm=tm, hf=hf: e.activation(out=tm["xs"][:, hf * 1024:(hf + 1) * 1024],
                                                                              in_=pBt[hf][:, :], func=AF.Copy),
                                 reads=[b_pB[hf]], writes=[tb["xs"]])
                        P.op("dve", lambda e, tm=tm, sm=sm: e.tensor_tensor(
                            out=tm["Xd"][:, :].rearrange("p (h d) -> p h d", d=HP),
                            in0=tm["X"][:, :].rearrange("p (h d) -> p h d", d=HP),
                            in1=sm[:, 3, :].unsqueeze(2).to_broadcast([128, NHS, HP]), op=ALU.mult),
                            reads=[tb["X"], tb["sm"]], writes=[tb["Xd"]])
                        for gg in range(NGR):
                            P.op("pe", lambda e, gg=gg, cs=cs: e.matmul(out=pD[0][:, gg * 128:(gg + 1) * 128],
                                                                         lhsT=xbcT[:, 16 + gg, cs], rhs=xbcT[:, 20 + gg, cs],
                                                                         start=True, stop=True),
                                 reads=[b_xbcj[16 + gg], b_xbcj[20 + gg]], writes=[b_pD[0]])
                        P.op("dve", lambda e, tm=tm: e.tensor_tensor(
                            out=tm["CBm"][:], in0=pD[0][:, :].rearrange("p (g l) -> p g l", l=128),
                            in1=cc(C_U).unsqueeze(1).to_broadcast([128, NGR, 128]), op=ALU.mult),
                            reads=[b_pD[0], b_cst], writes=[tb["CBm"]])
                        def s1(b, tm=tm, tb=tb, c=c):
                            jb = b % 3
                            Rb, LT = tm["Rb"][jb], tm["LT"][jb]
                            bRb, bLT = tb["Rb"][jb], tb["LT"][jb]
                            P.op("pool" if b % 2 == 0 else "dve", lambda e: e.tensor_tensor(
                                out=Rb[:], in0=cc(C_U).unsqueeze(1).to_broadcast([128, 4, 128]),
                                in1=dAt[:, c, 4 * b:4 * b + 4].unsqueeze(2).to_broadcast([128, 4, 128]), op=ALU.mult),
                                reads=[b_cst, b_dt], writes=[bRb])
                            pdj = (b + 1) % 2
                            P.op("pe", lambda e: e.matmul(out=pD[pdj][:, :], lhsT=cc(C_L),
                                                          rhs=Rb[:, :, :].rearrange("p h l -> p (h l)"), start=True, stop=True),
                                 reads=[bRb, b_cst], writes=[b_pD[pdj]])
                            P.op("act", lambda e: e.activation(out=LT[:, :, :].rearrange("p h l -> p (h l)"), in_=pD[pdj][:, :],
                                                               func=AF.Exp), reads=[b_pD[pdj]], writes=[bLT])

                        def s2(b, tm=tm, tb=tb):
                            jb = b % 3
                            gg = b // 2
                            LT, MT = tm["LT"][jb], tm["MT"][jb]
                            bLT, bMT = tb["LT"][jb], tb["MT"][jb]
                            P.op("dve", lambda e: e.tensor_tensor(
                                out=MT[:], in0=LT[:], in1=tm["CBm"][:, gg, :].unsqueeze(1).to_broadcast([128, 4, 128]),
                                op=ALU.mult), reads=[bLT, tb["CBm"]], writes=[bMT])
                            yb = gg % 2
                            for hb in range(4):
                                h = 4 * b + hb
                                P.op("pe", lambda e, hb=hb, h=h: e.matmul(
                                    out=pg(yb)[:, (h % 8) * HP:(h % 8 + 1) * HP], lhsT=MT[:, hb, :],
                                    rhs=tm["X"][:, h * HP:(h + 1) * HP], start=True, stop=True),
                                    reads=[bMT, tb["X"]], writes=[b_pg[yb]])

                        def ybank(gg):
                            if gg % 2 == 0:
                                return pg(2), b_pg[2], pg(3), b_pg[3]
                            return pBt[0][:, :].bitcast(F32), b_pB[0], pBt[1][:, :].bitcast(F32), b_pB[1]

                        def gpe(gg, tm=tm, tb=tb, cs=cs):
                            po, bpo, pst, bpst = ybank(gg)
                            P.op("pe", lambda e: e.matmul(out=po, lhsT=xbcT[:, 20 + gg, cs], rhs=Hbf[li][:, gg, :],
                                                          start=True, stop=True),
                                 reads=[b_xbcj[20 + gg], b_Hbf[li][gg]], writes=[bpo])
                            P.op("pe", lambda e: e.matmul(out=pst, lhsT=tm["Btok"][:, gg * 128:(gg + 1) * 128],
                                                          rhs=tm["Xd"][:, gg * 512:(gg + 1) * 512], start=True, stop=True),
                                 reads=[tb["Btok"], tb["Xd"]], writes=[bpst])

                        def comb(gg, tm=tm, tb=tb, sm=sm, c=c):
                            yb = gg % 2
                            po, bpo, pst, bpst = ybank(gg)
                            gsl = slice(gg * 512, (gg + 1) * 512)
                            P.op("dve", lambda e: e.tensor_tensor(
                                out=tm["y"][:, gsl].rearrange("p (h d) -> p h d", d=HP),
                                in0=po.rearrange("p (h d) -> p h d", d=HP),
                                in1=sm[:, 2, gg * 8:(gg + 1) * 8].unsqueeze(2).to_broadcast([128, 8, HP]), op=ALU.mult),
                                reads=[bpo, tb["sm"]], writes=[tb["yg"][gg]])
                            P.op("dve", lambda e: e.tensor_tensor(out=tm["y"][:, gsl], in0=tm["y"][:, gsl], in1=pg(yb),
                                                                  op=ALU.add), reads=[tb["yg"][gg], b_pg[yb]], writes=[tb["yg"][gg]])
                            P.op("dve", lambda e: e.tensor_tensor(out=tm["y"][:, gsl], in0=tm["y"][:, gsl],
                                                                  in1=zs[:, c, gsl], op=ALU.mult),
                                 reads=[tb["yg"][gg], b_zs], writes=[tb["yg"][gg]])
                            P.op("dve", lambda e: e.tensor_tensor(out=tm["y"][:, gsl], in0=tm["y"][:, gsl], in1=tm["Dx"][:, gsl],
                                                                  op=ALU.add), reads=[tb["yg"][gg], tb["Dx"]], writes=[tb["yg"][gg]])
                            P.op("act", lambda e: e.activation(out=tm["t1"][yb][:], in_=tm["y"][:, gsl], func=AF.Square,
                                                               accum_out=tm["ssq"][:, gg:gg + 1]),
                                 reads=[tb["yg"][gg]], writes=[tb["t1"][yb], tb["ssq"]])
                            P.op("dve", lambda e: e.tensor_tensor(
                                out=Hst[li][:, gg, :].rearrange("p (h d) -> p h d", d=HP),
                                in0=Hst[li][:, gg, :].rearrange("p (h d) -> p h d", d=HP),
                                in1=sm[:, 4, gg * 8:(gg + 1) * 8].unsqueeze(2).to_broadcast([128, 8, HP]), op=ALU.mult),
                                reads=[b_Hst[li][gg], tb["sm"]], writes=[b_Hst[li][gg]])
                            P.op("dve", lambda e: e.tensor_tensor(out=Hst[li][:, gg, :], in0=Hst[li][:, gg, :], in1=pst,
                                                                  op=ALU.add),
                                 reads=[bpst], writes=[b_Hst[li][gg]])
                            P.op("act", lambda e: e.activation(out=Hbf[li][:, gg, :], in_=Hst[li][:, gg, :], func=AF.Copy),
                                 reads=[b_Hst[li][gg]], writes=[b_Hbf[li][gg]])

                        s1(0)
                        s1(1)
                        for gg in range(NGR):
                            P.op("pe", lambda e, gg=gg, cs=cs: e.transpose(out=pBt[0][:, gg * 128:(gg + 1) * 128],
                                                                            in_=xbcT[:, 16 + gg, cs], identity=identb[:]),
                                 reads=[b_xbcj[16 + gg], b_setup], writes=[b_pB[0]])
                        P.op("act", lambda e, tm=tm: e.activation(out=tm["Btok"][:], in_=pBt[0][:, 0:512], func=AF.Copy),
                             reads=[b_pB[0]], writes=[tb["Btok"]])
                        def dx_ops(tm=tm, tb=tb, c=c):
                            P.op("dve", lambda e: e.tensor_tensor(
                                out=tm["Dx"][:, :].rearrange("p (h d) -> p h d", d=HP),
                                in0=tm["xs"][:, :].rearrange("p (h d) -> p h d", d=HP),
                                in1=pcol(l, PL_DSK, NHS).unsqueeze(2).to_broadcast([128, NHS, HP]), op=ALU.mult),
                                reads=[tb["xs"], b_pp], writes=[tb["Dx"]])
                            P.op("dve", lambda e: e.tensor_tensor(out=tm["Dx"][:], in0=tm["Dx"][:], in1=zs[:, c, :], op=ALU.mult),
                                 reads=[tb["Dx"], b_zs], writes=[tb["Dx"]])
                        yield
                        pend_g, pend_c = [], []
                        for t in range(2, 10):
                            if t < 8:
                                s1(t)
                            s2(t - 2)
                            if t == 3:
                                dx_ops()
                            for gg in pend_c:
                                comb(gg)
                            pend_c = []
                            for gg in pend_g:
                                gpe(gg)
                                pend_c.append(gg)
                            pend_g = []
                            if (t - 2) % 2 == 1:
                                pend_g.append((t - 2) // 2)
                        for gg in pend_c:
                            comb(gg)
                        for gg in pend_g:
                            gpe(gg)
                            comb(gg)

                        ast = tm["ast"]
                        mk = mask0b if gc == 0 else maskb

                        def a1(h, tm=tm, tb=tb, ast=ast, mk=mk, cs=cs, c=c):
                            hh, j, kv = h % 2, h // 2, h // 8
                            x = h % 2
                            sc, bsc = tm["sc"][h % 3], tb["sc"][h % 3]
                            pp_, bpp = tm["pp_"][h % 5], tb["pp_"][h % 5]
                            P.op("pe", lambda e: e.matmul(
                                out=pD[x][:, 0:256], lhsT=qT[hh * 64:(hh + 1) * 64, j, cs],
                                rhs=kT[li][hh * 64:(hh + 1) * 64, kv, c * T:c * T + 256], start=True, stop=False),
                                reads=[b_qT, b_kT[li]], writes=[b_pD[x]])
                            P.op("pe", lambda e: e.matmul(out=pD[x][:, 0:256], lhsT=identb[:], rhs=mk[:], start=False, stop=True),
                                 reads=[b_setup], writes=[b_pD[x]])
                            P.op("dve", lambda e: e.scalar_tensor_tensor(
                                out=sc[:], in0=cst[:, C_R0:C_R0 + 256], scalar=SLOPES[h], in1=pD[x][:, 0:256], op0=ALU.mult,
                                op1=ALU.add), reads=[b_cst, b_pD[x]], writes=[bsc])
                            P.op("dve", lambda e: e.reduce_max(out=ast[:, 0, h:h + 1], in_=sc[:], axis=AX.X),
                                 reads=[bsc], writes=[tb["asth"][h]])
                            P.op("dve", lambda e: e.tensor_scalar(
                                out=ast[:, 1, h:h + 1], in0=ast[:, 0, h:h + 1], scalar1=pcol(l, PL_SINK, 1, h), scalar2=-1.0,
                                op0=ALU.max, op1=ALU.mult), reads=[tb["asth"][h], b_pp], writes=[tb["asth"][h]])
                            P.op("act", lambda e: e.activation(
                                out=pp_[:], in_=sc[:], func=AF.Exp, bias=ast[:, 1, h:h + 1], accum_out=ast[:, 2, h:h + 1]),
                                reads=[bsc, tb["asth"][h]], writes=[bpp, tb["asth"][h]])

                        def a2(h, tm=tm, tb=tb):
                            x = h % 2
                            r = h % 5
                            pp_, bpp = tm["pp_"][r], tb["pp_"][r]
                            pT, bpT = tm["pT"][r], tb["pT"][r]
                            for half in range(2):
                                P.op("pe", lambda e, half=half: e.transpose(
                                    out=pBt[x][:, half * 128:(half + 1) * 128], in_=pp_[:, half * 128:(half + 1) * 128],
                                    identity=identb[:]), reads=[bpp, b_setup], writes=[b_pB[x]])
                            P.op("act", lambda e: e.activation(out=pT[:], in_=pBt[x][:, 0:256], func=AF.Copy),
                                 reads=[b_pB[x]], writes=[bpT])

                        def a3(h, tm=tm, tb=tb, c=c):
                            kv = h // 8
                            r = h % 5
                            pT, bpT = tm["pT"][r], tb["pT"][r]
                            for half in range(2):
                                P.op("pe", lambda e, half=half: e.matmul(
                                    out=pgB[:, h * 64:(h + 1) * 64], lhsT=pT[:, half * 128:(half + 1) * 128],
                                    rhs=vtok[li][:, c + half, kv * 64:(kv + 1) * 64], start=(half == 0), stop=(half == 1)),
                                    reads=[bpT, b_vtok[li]], writes=[b_pg[2 + h // 8]])

                        def rms_part1(tm=tm, tb=tb, c=c):
                            ssq = tm["ssq"]
                            P.op("dve", lambda e, ssq=ssq: e.tensor_scalar(out=ssq[:, 4:8], in0=ssq[:, 0:4], scalar1=1.0 / 512,
                                                                           scalar2=RMS_EPS, op0=ALU.mult, op1=ALU.add),
                                 reads=[tb["ssq"]], writes=[tb["ssq"]])
                            P.op("act", lambda e, ssq=ssq: e.activation(out=ssq[:, 4:8], in_=ssq[:, 4:8], func=AF.Sqrt),
                                 reads=[tb["ssq"]], writes=[tb["ssq"]])
                            P.op("dve", lambda e, ssq=ssq: e.reciprocal(out=ssq[:, 8:12], in_=ssq[:, 4:8]),
                                 reads=[tb["ssq"]], writes=[tb["ssq"]])
                            for gg in range(NGR):
                                P.op("act", lambda e, tm=tm, gg=gg, ssq=ssq: e.activation(
                                    out=tm["yn"][:, gg * 512:(gg + 1) * 512], in_=tm["y"][:, gg * 512:(gg + 1) * 512], func=AF.Copy,
                                    scale=ssq[:, 8 + gg:9 + gg]), reads=[tb["yg"][gg], tb["ssq"]], writes=[tb["yn"]])
                            if g == 0 and li == 0 and c == 0:
                                dump("y", tm["y"][:], tb["yg"], [128, DI])
                                dump("X", tm["X"][:], [tb["X"]], [128, DI], BF16)
                                dump("sm", tm["sm"][:], [tb["sm"]], [128, 5, NHS])
                                dump("CBm", tm["CBm"][:], [tb["CBm"]], [128, NGR, 128], BF16)
                                ckpt(3)

                        for t in range(AH + 4):
                            if t == 4:
                                rms_part1()
                            if t < AH:
                                a1(t)
                            if 0 <= t - 2 < AH:
                                a2(t - 2)
                            if 0 <= t - 4 < AH:
                                a3(t - 4)

                        yield
                        for cb in range(16):
                            P.op("pe", lambda e, cb=cb, tm=tm: e.transpose(
                                out=pBt[cb // 8][:, (cb % 8) * 128:(cb % 8 + 1) * 128], in_=tm["yn"][:, cb * 128:(cb + 1) * 128],
                                identity=identb[:]), reads=[tb["yn"], b_setup], writes=[b_pB[cb // 8]])
                        for hf in range(2):
                            P.op("dve", lambda e, hf=hf, cs=cs: e.tensor_tensor(
                                out=yT[:, hf * 8:(hf + 1) * 8, cs], in0=pBt[hf][:, :].rearrange("p (j t) -> p j t", t=128),
                                in1=pcol(l, PL_NORMW, 8, hf * 8).unsqueeze(2).to_broadcast([128, 8, 128]), op=ALU.mult),
                                reads=[b_pB[hf], b_pp], writes=[b_yT])

                        P.op("dve", lambda e, ast=ast: e.tensor_tensor(out=ast[:, 3, :], in0=ast[:, 1, :],
                                                                       in1=pcol(l, PL_SINK, AH), op=ALU.add),
                             reads=tb["asth"] + [b_pp], writes=[tb["ast"]])
                        P.op("act", lambda e, ast=ast: e.activation(out=ast[:, 3, :], in_=ast[:, 3, :], func=AF.Exp),
                             reads=[tb["ast"]], writes=[tb["ast"]])
                        P.op("dve", lambda e, ast=ast: e.tensor_tensor(out=ast[:, 4, :], in0=ast[:, 3, :], in1=ast[:, 2, :],
                                                                       op=ALU.add), reads=[tb["ast"]], writes=[tb["ast"]])
                        P.op("dve", lambda e, ast=ast: e.reciprocal(out=ast[:, 5, :], in_=ast[:, 4, :]),
                             reads=[tb["ast"]], writes=[tb["ast"]])
                        for hf in range(2):
                            P.op("dve", lambda e, tm=tm, ast=ast, hf=hf: e.tensor_tensor(
                                out=tm["att"][:, hf * 512:(hf + 1) * 512].rearrange("p (h d) -> p h d", d=64),
                                in0=pgB[:, hf * 512:(hf + 1) * 512].rearrange("p (h d) -> p h d", d=64),
                                in1=ast[:, 5, hf * 8:(hf + 1) * 8].unsqueeze(2).to_broadcast([128, 8, 64]), op=ALU.mult),
                                reads=[b_pg[2 + hf], tb["ast"]], writes=[tb["att"]])
                        if g == 0 and li == 0 and c == 0:
                            dump("att", tm["att"][:], [tb["att"]], [128, D], BF16)
                            ckpt(4)
                        for j in range(8):
                            P.op("pe", lambda e, tm=tm, j=j: e.transpose(out=pBt[0][:, j * 128:(j + 1) * 128],
                                                                         in_=tm["att"][:, j * 128:(j + 1) * 128],
                                                                         identity=identb[:]),
                                 reads=[tb["att"], b_setup], writes=[b_pB[0]])
                        P.op("act", lambda e, cs=cs: e.activation(out=attT[:, :, cs],
                                                                  in_=pBt[0][:, :].rearrange("p (j t) -> p j t", t=128),
                                                                  func=AF.Copy), reads=[b_pB[0]], writes=[b_attT])
                gens = [chunk_gen(c) for c in range(NCH)]
                next(gens[0])
                for c in range(NCH):
                    next(gens[c])
                    if c + 1 < NCH:
                        next(gens[c + 1])
                    for _ in gens[c]:
                        pass

                P.op("dve", lambda e: e.tensor_copy(out=kT[li][:, :, 0:T], in_=kT[li][:, :, NT:NT + T]),
                     reads=[b_kT[li]], writes=[b_kT[li]])
                P.op("dve", lambda e: e.tensor_copy(out=vtok[li][:, 0, :], in_=vtok[li][:, NCH, :]),
                     reads=[b_vtok[li]], writes=[b_vtok[li]])
                if g == 0 and li == 0:
                    dump("yT", yT[:], [b_yT], [128, 16, NT], BF16)
                    dump("attT", attT[:], [b_attT], [128, 8, NT], BF16)
                    ckpt(5)

                for j in range(8):
                    slotA = cur.next()
                    k0, k1, k2, k3 = nextbank(), nextbank(), nextbank(), nextbank()
                    gemm_b(slotA, 0, 8, hT, b_hTj, k0)
                    gemm_b(slotA, 1024, 8, hT, b_hTj, k1)
                    gemm_b(slotA, 2048, 8, attT, b_attT, k2)
                    cur.done()
                    slotB = cur.next()
                    gemm_b(slotB, 0, 16, yT, b_yT, k3)
                    cur.done()
                    P.op("act", lambda e: e.activation(out=gtmp[0][:], in_=pg(k0)[:, 0:NT], func=AF.Sigmoid),
                         reads=[b_pg[k0]], writes=[b_gtmp[0]])
                    P.op("act", lambda e: e.activation(out=gtmp[1][:], in_=pg(k1)[:, 0:NT], func=AF.Sigmoid),
                         reads=[b_pg[k1]], writes=[b_gtmp[1]])
                    P.op("dve", lambda e: e.tensor_tensor(out=gtmp[2][:], in0=gtmp[0][:], in1=pg(k3)[:, 0:NT], op=ALU.mult),
                         reads=[b_gtmp[0], b_pg[k3]], writes=[b_gtmp[2]])
                    P.op("dve", lambda e: e.tensor_tensor(out=gtmp[3][:], in0=gtmp[1][:], in1=pg(k2)[:, 0:NT], op=ALU.mult),
                         reads=[b_gtmp[1], b_pg[k2]], writes=[b_gtmp[3]])
                    P.op("dve", lambda e, j=j: e.tensor_tensor(out=mixT[:, j, :], in0=gtmp[2][:], in1=gtmp[3][:],
                                                               op=ALU.add),
                         reads=[b_gtmp[2], b_gtmp[3]], writes=[b_mixT])
                if g == 0 and li == 0:
                    dump("mixT", mixT[:], [b_mixT], [128, 8, NT], BF16)
                    ckpt(6)

                for s in range(2):
                    slot = cur.next()
                    for bi in range(4):
                        j = s * 4 + bi
                        bank = nextbank()
                        gemm_b(slot, bi * 1024, 8, mixT, b_mixT, bank)
                        P.op("dve", lambda e, j=j, bank=bank: e.scalar_tensor_tensor(
                            out=rT[:, j, :], in0=hresT[:, j, :], scalar=ALPHA, in1=pg(bank)[:, 0:NT], op0=ALU.mult,
                            op1=ALU.add), reads=[b_hres, b_pg[bank]], writes=[b_rTj[j]])
                        ln_pre(j)
                    cur.done()
                ln_feature(l, PL_LMG, PL_LMB)
                if g == 0 and li == 0:
                    dump("h1T", hresT[:], [b_hres], [128, KD, NT])
                    ckpt(7)

                for s in range(11):
                    slot = cur.next()
                    for bi in range(2):
                        i = s * 2 + bi
                        kg, ku = nextbank(), nextbank()
                        gemm_b(slot, bi * 2048, 8, hT, b_hTj, kg)
                        gemm_b(slot, bi * 2048 + 1024, 8, hT, b_hTj, ku)
                        x = i % 4
                        P.op("act", lambda e, x=x: e.activation(out=gtmp[x][:], in_=pg(kg)[:, 0:NT], func=AF.Silu),
                             reads=[b_pg[kg]], writes=[b_gtmp[x]])
                        P.op("dve", lambda e, x=x, i=i: e.tensor_tensor(out=hidT[:, i, :], in0=gtmp[x][:],
                                                                        in1=pg(ku)[:, 0:NT], op=ALU.mult),
                             reads=[b_gtmp[x], b_pg[ku]], writes=[b_hid[i]])
                    cur.done()
                if li == NL - 1 and g + 1 < ngroups:
                    p0a(g + 1)
                for j in range(8):
                    slot = cur.next()
                    bank = nextbank()
                    gemm_b(slot, 0, KF, hidT, b_hid, bank)
                    P.op("dve", lambda e, j=j, bank=bank: e.scalar_tensor_tensor(
                        out=rT[:, j, :], in0=hresT[:, j, :], scalar=ALPHA, in1=pg(bank)[:, 0:NT], op0=ALU.mult,
                        op1=ALU.add), reads=[b_hres, b_pg[bank]], writes=[b_rTj[j]])
                    ln_pre(j)
                    cur.done()
                ln_feature(l, PL_LFG, PL_LFB)
                if g == 0 and li == 0:
                    dump("h2T", hresT[:], [b_hres], [128, KD, NT])

            for c in range(NCH):
                gc = g * NCH + c
                ob, bo = osb[0], b_osb[0]
                for j in range(KD):
                    P.op("pe", lambda e, j=j, c=c: e.transpose(out=pgA[:, j * 128:(j + 1) * 128],
                                                               in_=hresT[:, j, c * T:(c + 1) * T], identity=cc(C_ID)),
                         reads=[b_hres, b_cst], writes=[b_pg[j // 4]])
                P.op("act", lambda e, ob=ob: e.activation(out=ob[:, 0:512], in_=pgA[:, 0:512], func=AF.Copy),
                     reads=[b_pg[0]], writes=[bo])
                P.op("dve", lambda e, ob=ob: e.tensor_copy(out=ob[:, 512:1024], in_=pgA[:, 512:1024]),
                     reads=[b_pg[1]], writes=[bo])
                byo = Buf(f"yo{gc}")
                P.op("pool", lambda e, ob=ob, gc=gc: e.dma_start(out=yout[gc * T:(gc + 1) * T, :], in_=ob),
                     reads=[bo], writes=[byo], dma=True)
                out_bufs.append(byo)

        P.op("sp", lambda e: e.nop(), reads=out_bufs)
        P.emit(st)
    return nc, dbg_out


_CACHE = {}


def _run(layers, first, x_list, inp, nch=2, debug=None):
    ntok = x_list[0].shape[0]
    key = (tuple(layers), first, ntok, nch, None if debug is None else tuple(debug))
    if key not in _CACHE:
        _CACHE[key] = build_program(layers, first, ntok // T, nch=nch, debug=debug)
    nc, dbg = _CACHE[key]
    wkeys = ("w_in", "w_ssd_out", "w_att_out", "w_mix_out", "w_ffn_gate", "w_ffn_up", "w_ffn_down")
    ws = [build_wstream(l, *[inp[k] for k in wkeys]) for l in layers]
    pp = build_params(inp)
    cst = build_consts()
    in_maps = []
    for xl in x_list:
        m = {"xin": np.ascontiguousarray(xl, dtype=np.float32), "pp": pp, "cst": cst}
        for i in range(len(layers)):
            m[f"wst{i}"] = ws[i]
        in_maps.append(m)
    res = run_bass_kernel_spmd(nc, in_maps, core_ids=list(range(len(x_list))))
    return res.results


def kernel(**inputs):
    inp = {k: np.asarray(v) for k, v in inputs.items()}
    x = inp["x"].astype(np.float32)
    xs = [x[b] for b in range(BATCH)]
    res = _run([0, 1], True, xs, inp)
    out = np.stack([r["yout"] for r in res], axis=0)
    return out.astype(np.float32)
```
